# Optimizing a Trainium2 kernel written in Bass

```python
import jax, jax.numpy as jnp
from jax import lax
import numpy as np

D_MODEL = 1024
BATCH = 2
SEQ = 8192
DEPTH = 1

CHUNK = 64
EPS = 1e-6

ATT_HEADS = 16
ATT_KV_HEADS = 2
HEAD_DIM = 64
ATT_GROUP = ATT_HEADS // ATT_KV_HEADS
ATT_WIDTH = ATT_HEADS * HEAD_DIM
KV_WIDTH = ATT_KV_HEADS * HEAD_DIM
WINDOW = 128
WINDOW_CHUNKS = WINDOW // CHUNK
KEY_BLOCK = (WINDOW_CHUNKS + 1) * CHUNK
ROPE_DIM = HEAD_DIM // 4
ROPE_THETA = 500000.0

SG_BLOCK = 128
SG_GROUPS = 8
SG_WIDTH = 1024
SG_GROUP_WIDTH = SG_WIDTH // SG_GROUPS

N_BRANCHES = 2
IN_SIZES = [ATT_WIDTH, KV_WIDTH, KV_WIDTH, ATT_WIDTH,
            SG_WIDTH, SG_WIDTH, SG_WIDTH,
            N_BRANCHES * D_MODEL]
IN_WIDTH = int(sum(IN_SIZES))
IN_SPLITS = [int(s) for s in np.cumsum(IN_SIZES)[:-1]]

kernel_name = "hybrid_swa_sink_gmlp_gated_block"


def rmsnorm(x, g):
    xf = x.astype(jnp.float32)
    y = xf * lax.rsqrt(jnp.mean(xf * xf, axis=-1, keepdims=True) + EPS)
    return (y * g.astype(jnp.float32)).astype(x.dtype)


def layernorm(x, g, b):
    xf = x.astype(jnp.float32)
    mu = jnp.mean(xf, axis=-1, keepdims=True)
    xc = xf - mu
    y = xc * lax.rsqrt(jnp.mean(xc * xc, axis=-1, keepdims=True) + EPS)
    return (y * g.astype(jnp.float32) + b.astype(jnp.float32)).astype(x.dtype)


def partial_rope(x, pos):
    half = ROPE_DIM // 2
    inv_freq = ROPE_THETA ** (-(jnp.arange(half, dtype=jnp.float32) * 2.0) / ROPE_DIM)
    ang = pos.astype(jnp.float32)[:, None] * inv_freq[None, :]
    cos = jnp.cos(ang)[None, :, None, :]
    sin = jnp.sin(ang)[None, :, None, :]
    xf = x.astype(jnp.float32)
    x1, x2, rest = xf[..., :half], xf[..., half:ROPE_DIM], xf[..., ROPE_DIM:]
    out = jnp.concatenate([x1 * cos - x2 * sin, x2 * cos + x1 * sin, rest], axis=-1)
    return out.astype(x.dtype)


def window_sink_attention(q, k, v, sinks):
    B, S = q.shape[0], q.shape[1]
    nc = S // CHUNK
    qc = q.reshape(B, nc, CHUNK, ATT_KV_HEADS, ATT_GROUP, HEAD_DIM)
    pad = ((0, 0), (WINDOW_CHUNKS, 0), (0, 0), (0, 0), (0, 0))
    kc = jnp.pad(k.reshape(B, nc, CHUNK, ATT_KV_HEADS, HEAD_DIM), pad)
    vc = jnp.pad(v.reshape(B, nc, CHUNK, ATT_KV_HEADS, HEAD_DIM), pad)
    kb = jnp.concatenate([kc[:, j:j + nc] for j in range(WINDOW_CHUNKS + 1)], axis=2)
    vb = jnp.concatenate([vc[:, j:j + nc] for j in range(WINDOW_CHUNKS + 1)], axis=2)
    s = jnp.einsum('bnqhgd,bnkhd->bhgnqk', qc, kb,
                   preferred_element_type=jnp.float32) * (HEAD_DIM ** -0.5)
    key_chunk = jnp.arange(nc)[:, None] - WINDOW_CHUNKS + jnp.arange(WINDOW_CHUNKS + 1)[None, :]
    valid = jnp.repeat(key_chunk >= 0, CHUNK, axis=1)
    s = jnp.where(valid[None, None, None, :, None, :], s, -1e30)
    sink = sinks.astype(jnp.float32).reshape(ATT_KV_HEADS, ATT_GROUP)[None, :, :, None, None, None]
    m = jnp.maximum(jnp.max(s, axis=-1, keepdims=True), sink)
    p = jnp.exp(s - m)
    denom = jnp.sum(p, axis=-1, keepdims=True) + jnp.exp(sink - m)
    o = jnp.einsum('bhgnqk,bnkhd->bnqhgd', p / denom, vb.astype(jnp.float32))
    return o.reshape(B, S, ATT_WIDTH).astype(q.dtype)


def chunked_spatial_gating(u, v, w_s, b_s, ln_g, ln_b):
    B, S = u.shape[0], u.shape[1]
    nb = S // SG_BLOCK
    v = layernorm(v, ln_g, ln_b)
    vb = v.reshape(B, nb, SG_BLOCK, SG_GROUPS, SG_GROUP_WIDTH)
    cidx = np.arange(SG_BLOCK) // CHUNK
    mask = jnp.asarray(cidx[None, :] <= cidx[:, None])
    w = jnp.where(mask[None], w_s, jnp.zeros_like(w_s))
    mixed = jnp.einsum('gij,bnjgc->bnigc', w, vb) + jnp.transpose(b_s)[None, None, :, :, None]
    return u * mixed.reshape(B, S, SG_WIDTH)


def setup_inputs(seed: int = 0) -> dict:
    key = jax.random.key(seed)
    ks = jax.random.split(key, 16)
    f32 = jnp.float32
    d = D_MODEL
    x = jax.random.normal(ks[0], (BATCH, SEQ, d), f32)
    norm_g = 1.0 + 0.05 * jax.random.normal(ks[1], (DEPTH, d), f32)
    w_in = jax.random.normal(ks[2], (DEPTH, d, IN_WIDTH), f32) * d ** -0.5
    b_merge = 0.01 * jax.random.normal(ks[3], (DEPTH, N_BRANCHES * d), f32)
    att_sinks = 0.5 * jax.random.normal(ks[4], (DEPTH, ATT_HEADS), f32)
    sg_w = jax.random.normal(ks[5], (DEPTH, SG_GROUPS, SG_BLOCK, SG_BLOCK), f32) * (0.5 * SG_BLOCK ** -0.5)
    sg_b = 1.0 + 0.1 * jax.random.normal(ks[6], (DEPTH, SG_GROUPS, SG_BLOCK), f32)
    sg_ln_g = 1.0 + 0.05 * jax.random.normal(ks[7], (DEPTH, SG_WIDTH), f32)
    sg_ln_b = 0.02 * jax.random.normal(ks[8], (DEPTH, SG_WIDTH), f32)
    w_att_out = jax.random.normal(ks[9], (DEPTH, ATT_WIDTH, d), f32) * ATT_WIDTH ** -0.5
    w_sg_out = jax.random.normal(ks[10], (DEPTH, SG_WIDTH, d), f32) * SG_WIDTH ** -0.5
    w_o = jax.random.normal(ks[11], (DEPTH, d, d), f32) * d ** -0.5
    final_g = 1.0 + 0.05 * jax.random.normal(ks[12], (d,), f32)
    return {"x": x, "norm_g": norm_g, "w_in": w_in, "b_merge": b_merge,
            "att_sinks": att_sinks, "sg_w": sg_w, "sg_b": sg_b, "sg_ln_g": sg_ln_g,
            "sg_ln_b": sg_ln_b, "w_att_out": w_att_out, "w_sg_out": w_sg_out,
            "w_o": w_o, "final_g": final_g}


def reference(x, norm_g, w_in, b_merge, att_sinks, sg_w, sg_b, sg_ln_g, sg_ln_b,
              w_att_out, w_sg_out, w_o, final_g):
    B, S = x.shape[0], x.shape[1]
    pos = jnp.arange(S)
    for l in range(DEPTH):
        h = rmsnorm(x, norm_g[l])
        proj = h @ w_in[l]
        q, k, v, za, u, vs, zs, gm = jnp.split(proj, IN_SPLITS, axis=-1)
        q = partial_rope(q.reshape(B, S, ATT_HEADS, HEAD_DIM), pos)
        k = partial_rope(k.reshape(B, S, ATT_KV_HEADS, HEAD_DIM), pos)
        v = v.reshape(B, S, ATT_KV_HEADS, HEAD_DIM)
        a = window_sink_attention(q, k, v, att_sinks[l])
        a = (a * jax.nn.silu(za)) @ w_att_out[l]
        sgo = chunked_spatial_gating(jax.nn.gelu(u, approximate=False),
                                     jax.nn.gelu(vs, approximate=False),
                                     sg_w[l], sg_b[l], sg_ln_g[l], sg_ln_b[l])
        sgo = (sgo * jax.nn.silu(zs)) @ w_sg_out[l]
        ga, gs = jnp.split(jax.nn.sigmoid(gm + b_merge[l]), N_BRANCHES, axis=-1)
        y = ga * a + gs * sgo
        x = x + y @ w_o[l]
    return rmsnorm(x, final_g)
```

```python
import numpy as np
from contextlib import ExitStack
import concourse.bass as bass
import concourse.mybir as mybir
from concourse.bass_utils import run_bass_kernel_spmd

F32 = mybir.dt.float32
BF16 = mybir.dt.bfloat16
AF = mybir.ActivationFunctionType
ALU = mybir.AluOpType

D = 1024
NCORES = 8
TOK_PER_CORE = 2048
EPS = 1e-6
NSLOT = 5
NVR = 6


class Res:
    __slots__ = ("name", "w", "r", "dsem", "dcnt")

    def __init__(self, name):
        self.name = name
        self.w = []
        self.r = []
        self.dsem = None
        self.dcnt = 0


class _Rec:
    def __init__(self):
        self.call = None

    def __getattr__(self, name):
        def f(*a, **k):
            self.call = (name, a, k)
            return self
        return f


def _bind(fn):
    rec = _Rec()
    fn(rec)
    name, a, k = rec.call
    return lambda e: getattr(e, name)(*a, **k)


class Emitter:
    ENGS = ("pe", "act", "dve", "pool", "sp")

    def __init__(self, nc, stack):
        self.nc = nc
        self.stack = stack
        self.prog = {e: [] for e in self.ENGS}
        self.esem = {}
        self.ecnt = {e: 0 for e in self.ENGS}
        self.waited = {e: {} for e in self.ENGS}
        self.sems = {}
        for e in self.ENGS:
            s = stack.enter_context(nc.semaphore("s_" + e))
            self.esem[e] = s
            self.sems[("e", e)] = s
        self.nres = 0
        self.stage = "setup"
        self.labels = {e: [] for e in self.ENGS}

    def res(self, name=None):
        self.nres += 1
        return Res(f"{name or 'r'}_{self.nres}")

    def _dsem(self, r):
        if r.dsem is None:
            s = self.stack.enter_context(self.nc.semaphore("d_" + r.name))
            r.dsem = ("d", r.name)
            self.sems[r.dsem] = s
        return r.dsem

    def _wait(self, eng, toks):
        wd = self.waited[eng]
        need = {}
        own = ("e", eng)
        for k, v in toks:
            if k == own and (eng == "pe" or v > self.ecnt[eng]):
                continue
            if wd.get(k, 0) < v and need.get(k, 0) < v:
                need[k] = v
        for k, v in need.items():
            wd[k] = v
            sem = self.sems[k]
            self.prog[eng].append(lambda e, sem=sem, v=v: e.wait_ge(sem, v))

    def _deps(self, reads, writes, eng):
        toks = []
        for r in reads:
            toks += r.w
        for w in writes:
            toks += w.w
            toks += [t for t in w.r if t[0] != ("e", eng)]
        return toks

    def op(self, eng, fn, reads=(), writes=()):
        self._wait(eng, self._deps(reads, writes, eng))
        self.ecnt[eng] += 1
        tok = (("e", eng), self.ecnt[eng])
        sem = self.esem[eng]
        fn = _bind(fn)
        self.labels[eng].append(self.stage)
        self.prog[eng].append(lambda e, fn=fn, sem=sem: fn(e).then_inc(sem, 1))
        for r in reads:
            r.r.append(tok)
        for w in writes:
            w.w = [tok]
            w.r = []
        return tok

    def op_noinc(self, eng, fn, reads=(), writes=()):
        self._wait(eng, self._deps(reads, writes, eng))
        fn = _bind(fn)
        self.labels[eng].append(self.stage)
        self.prog[eng].append(lambda e, fn=fn: fn(e))
        tok = (("e", eng), self.ecnt[eng] + 1)
        for r in reads:
            r.r.append(tok)
        for w in writes:
            w.w = [tok]
            w.r = []

    def dma(self, eng, out, in_, reads=(), writes=(), part=False):
        self._wait(eng, self._deps(reads, () if part else writes, eng))
        sr = writes[0] if writes else reads[0]
        k = self._dsem(sr)
        sr.dcnt += 16
        tok = (k, sr.dcnt)
        sem = self.sems[k]
        self.prog[eng].append(
            lambda e, out=out, in_=in_, sem=sem: e.dma_start(out=out, in_=in_).then_inc(sem, 16))
        for r in reads:
            r.r.append(tok)
        for w in writes:
            w.w = [tok]
            w.r = []
        return tok

    def wait_all(self, eng, ress):
        toks = []
        for r in ress:
            toks += r.w + r.r
        self._wait(eng, toks)

    def finish(self):
        with self.nc.Block() as block:
            def run(name):
                def f(e):
                    for c in self.prog[name]:
                        c(e)
                return f
            block.tensor(run("pe"))
            block.scalar(run("act"))
            block.vector(run("dve"))
            block.gpsimd(run("pool"))
            block.sync(run("sp"))


def build(NBLK=4, debug=False, stop=None):
    NT = NBLK * 4
    nc = bass.Bass("TRN2", target_bir_lowering=False)

    def din(name, shape):
        return nc.dram_tensor(name, shape, F32, kind="ExternalInput").ap()

    x_d = din("x", [NT * 128, D])
    xh_d = din("xh", [128, D])
    cs_d = din("cs", [128, NT + 1, 16])
    valid_d = din("valid", [128, 1])
    ident_d = din("ident", [128, 128])
    win_d = din("w_in", [D, 7424])
    wao_d = din("w_att_out", [D, D])
    wso_d = din("w_sg_out", [D, D])
    wo_d = din("w_o", [D, D])
    ng_d = din("norm_g", [1, D])
    fg_d = din("final_g", [1, D])
    bm_d = din("bm", [128, 16])
    lng_d = din("lng", [128, 8])
    lnb_d = din("lnb", [128, 8])
    snk_d = din("sinks", [128, 16])
    wsT_d = din("wsT", [128, 8, 128])
    sgb_d = din("sgb", [1, 1024])
    out_d = nc.dram_tensor("out", [NT * 128, D], F32, kind="ExternalOutput").ap()
    dbg_d = {}

    with ExitStack() as st:
        em = Emitter(nc, st)

        def sb(name, shape, dt=F32):
            return st.enter_context(nc.sbuf_tensor("sb_" + name, shape, dt)), em.res(name)

        identb, r_identb = sb("identb", [128, 128], BF16)
        gbc, r_gbc = sb("gbc", [128, D])
        fgbc, r_fgbc = sb("fgbc", [128, D])
        bm, r_bm = sb("bm", [128, 16])
        lng, r_lng = sb("lng", [128, 8])
        lnb, r_lnb = sb("lnb", [128, 8])
        snk, r_snk = sb("snk", [128, 16])
        esk, r_esk = sb("esk", [128, 16])
        esrhs, r_esrhs = sb("esrhs", [128, 512], BF16)
        sel2, r_sel2 = sb("sel2", [128, 128], BF16)
        wsTb, r_wsTb = sb("wsTb", [128, 8, 128], BF16)
        Cc, r_Cc = sb("Cc", [128, 8, 128])
        cs, r_cs = sb("cs", [128, NT + 1, 16])
        validt, r_valid = sb("valid", [128, 1])
        epsc, r_epsc = sb("epsc", [128, 1])
        onesb, r_onesb = sb("onesb", [128, 128], BF16)
        onesp, r_onesp = sb("onesp", [128, 2, 128], BF16)
        onesph, r_onesph = sb("onesph", [128, 2, 128], BF16)

        kTp, r_kTall = sb("kTp", [128, NVR, 2, 128], BF16)
        r_kT = [em.res("kT") for _ in range(NVR)]
        VA, r_VAall = sb("VA", [128, NVR, 2, 128], BF16)
        r_VA = [em.res("VA") for _ in range(NVR)]
        xnT = []
        for i in range(2):
            xnT.append(sb(f"xnT{i}", [128, 8, 512], BF16))
        xnTh, r_xnTh = sb("xnTh", [128, 8, 128], BF16)
        xt = [sb(f"xt{i}", [128, D]) for i in range(4)]
        xn = [sb(f"xn{i}", [128, D], BF16) for i in range(2)]
        sqj, r_sqj = sb("sqj", [128, D], BF16)
        stt = [sb(f"stt{i}", [128, 8]) for i in range(4)]
        A1, r_A1 = sb("A1", [128, 8, 512], BF16)
        A2, r_A2 = sb("A2", [128, 8, 512], BF16)
        gaT, r_gaT = sb("gaT", [128, 8, 512], BF16)
        bmh, r_bmh = sb("bmh", [128, 16])
        A3, r_A3 = sb("A3", [128, 8, 512], BF16)
        qb = [sb(f"qb{i}", [128, 1280], BF16) for i in range(1)]
        rt = [sb(f"rt{i}", [128, 4, 144]) for i in range(1)]
        gv = [sb(f"gv{i}", [128, D]) for i in range(2)]
        bst = [sb(f"bst{i}", [128, 2, 6]) for i in range(2)]
        f5 = {k: [sb(f"{k}{i}", [128, 512]) for i in range(2)] for k in ("msb", "gu", "szs", "gg")}
        f5["dn"] = f5["msb"]
        f5["oo"] = f5["gu"]
        PT1 = [(gv[i][0][:, 0:512].bitcast(BF16).rearrange("p (g c) -> p g c", g=2), gv[i][1]) for i in range(2)]
        PT2 = [(gv[i][0][:, 512:1024].bitcast(BF16).rearrange("p (g c) -> p g c", g=2), gv[i][1]) for i in range(2)]
        identf, r_identf = gv[1][0][:, 0:128], gv[1][1]
        wsTf, r_wsTf = gv[0][0][:, :].rearrange("p (g i) -> p g i", i=128), gv[0][1]
        wslot = [sb(f"ws{i}", [128, 8, 512], BF16) for i in range(NSLOT)]
        wres = {k: sb("wr_" + k, [128, 8, n], BF16) for k, n in (("q", 1024), ("kv", 256), ("za0", 512), ("za1", 512))}

        def permview(t, kc):
            return t[:, kc, :].rearrange("p (kv hg d) -> p hg kv d", kv=2, hg=8)

        psA = st.enter_context(nc.psum_tensor("psA", [128, 8, 512], F32))
        pT6 = psA[:, 6, :].bitcast(BF16)
        pT7 = psA[:, 7, :].bitcast(BF16)
        r_bank = [em.res(f"bank{i}") for i in range(8)]
        fbc = [0]

        def fb():
            b = fbc[0] % 6
            fbc[0] += 1
            return b

        if debug:
            for nm, shp, dt in (("xnT", [128, 8, 512], BF16), ("qT", [128, 8, 512], BF16),
                                ("sza", [128, 8, 512], BF16), ("aT", [128, 8, 512], BF16), ("yaT", [128, 8, 512], BF16),
                                ("sT", [128, 8, 512], BF16), ("yT", [128, 8, 512], BF16), ("VA", [128, NVR, 2, 128], BF16)):
                dbg_d[nm] = nc.dram_tensor("dbg_" + nm, shp, dt, kind="ExternalOutput").ap()
        r_dbg = em.res("dbg")

        dumped = []

        def dump(nm, t, r):
            if debug:
                em.dma("sp", dbg_d[nm], t[:], reads=[r])
                dumped.append(r)

        em.dma("sp", identf, ident_d, writes=[r_identf])
        em.dma("sp", gbc[:], ng_d.partition_broadcast(128), writes=[r_gbc])
        em.op("dve", lambda e: e.tensor_copy(out=identb[:], in_=identf), reads=[r_identf], writes=[r_identb])
        em.op("dve", lambda e: e.memset(epsc[:], EPS), writes=[r_epsc])
        def late_setup():
            em.dma("sp", wsTf, wsT_d, writes=[r_wsTf])
            for (t, r, d) in ((cs, r_cs, cs_d), (validt, r_valid, valid_d), (bm, r_bm, bm_d), (lng, r_lng, lng_d), (lnb, r_lnb, lnb_d),
                              (snk, r_snk, snk_d)):
                em.dma("sp", t[:], d, writes=[r])
            em.dma("sp", fgbc[:], fg_d.partition_broadcast(128), writes=[r_fgbc])
            bsbc = xt[3][0]
            em.dma("sp", bsbc[:], sgb_d.partition_broadcast(128), writes=[xt[3][1]])
            em.op("dve", lambda e: e.memset(onesb[:], 1.0), writes=[r_onesb])
            em.op("dve", lambda e: e.tensor_scalar(out=bmh[:], in0=bm[:], scalar1=0.5, scalar2=None, op0=ALU.mult), reads=[r_bm], writes=[r_bmh])
            em.op("dve", lambda e: e.memset(onesp[:], 0.0), writes=[r_onesp])
            em.op("dve", lambda e: e.memset(onesp[:, 0, 0:64], 1.0), writes=[r_onesp])
            em.op("dve", lambda e: e.memset(onesp[:, 1, 64:128], 1.0), writes=[r_onesp])
            em.op("dve", lambda e: e.tensor_scalar(out=onesph[:], in0=onesp[:], scalar1=validt[:, 0:1], scalar2=None, op0=ALU.mult),
                  reads=[r_onesp, r_valid], writes=[r_onesph])
            em.op("dve", lambda e: e.memset(VA[:], 0.0), writes=r_VA)
            em.op("dve", lambda e: e.memset(kTp[:], 0.0), writes=r_kT)
            em.op("act", lambda e: e.activation(out=esk[:], in_=snk[:], func=AF.Exp), reads=[r_snk], writes=[r_esk])
            es_hb = xn[0][0][:, 0:16]
            es_tmp, r_es_tmp = xt[2][0], xt[2][1]
            em.op("dve", lambda e: e.tensor_copy(out=es_hb, in_=esk[:]), reads=[r_esk], writes=[xn[0][1]])
            em.op("dve", lambda e: e.tensor_copy(out=es_tmp[:, 0:16], in_=es_hb), reads=[xn[0][1]], writes=[r_es_tmp])
            em.op("dve", lambda e: e.tensor_tensor(out=es_tmp[:, 16:32], in0=esk[:], in1=es_tmp[:, 0:16], op=ALU.subtract),
                  reads=[r_esk, r_es_tmp], writes=[r_es_tmp])
            em.op("dve", lambda e: e.memset(esrhs[:], 0.0), writes=[r_esrhs])
            for g in range(2):
                p_hi, p_lo = 64 * g, 64 * g + 32
                em.op("dve", lambda e: e.tensor_copy(out=esrhs[p_hi:p_hi + 1, :].rearrange("p (h q) -> p h q", q=64),
                                                     in_=es_tmp[p_hi:p_hi + 1, 8 * g:8 * g + 8].unsqueeze(2).to_broadcast([1, 8, 64])),
                      reads=[r_es_tmp], writes=[r_esrhs])
                em.op("dve", lambda e: e.tensor_copy(out=esrhs[p_lo:p_lo + 1, :].rearrange("p (h q) -> p h q", q=64),
                                                     in_=es_tmp[p_lo:p_lo + 1, 16 + 8 * g:16 + 8 * g + 8].unsqueeze(2).to_broadcast([1, 8, 64])),
                      reads=[r_es_tmp], writes=[r_esrhs])
            em.op("dve", lambda e: e.memset(sel2[:], 0.0), writes=[r_sel2])
            em.op("dve", lambda e: e.memset(sel2[0:64, 0:64], 1.0), writes=[r_sel2])
            em.op("dve", lambda e: e.memset(sel2[64:128, 64:128], 1.0), writes=[r_sel2])
            em.op("dve", lambda e: e.tensor_copy(out=wsTb[:], in_=wsTf), reads=[r_wsTf], writes=[r_wsTb])
            em.op("dve", lambda e: e.memset(wsTb[64:128, :, 0:64], 0.0), writes=[r_wsTb])
            for g in range(8):
                b = fb()
                em.op("pe", lambda e, b=b, g=g: e.matmul(psA[:, b, 0:128], lhsT=onesb[:], rhs=wsTb[:, g, :], start=True, stop=True),
                      reads=[r_onesb, r_wsTb], writes=[r_bank[b]])
                em.op("dve", lambda e, b=b, g=g: e.scalar_tensor_tensor(out=Cc[:, g, :], in0=psA[:, b, 0:128], scalar=lnb[:, g:g + 1],
                                                                        in1=bsbc[:, g * 128:(g + 1) * 128], op0=ALU.mult, op1=ALU.add),
                      reads=[r_bank[b], r_lnb, xt[3][1]], writes=[r_Cc])


        winv = win_d.rearrange("(kc p) n -> p kc n", p=128)
        wsov = wso_d.rearrange("(kc p) n -> p kc n", p=128)
        wov = wo_d.rearrange("(kc p) n -> p kc n", p=128)
        slot_ctr = [0]

        def load_unit(kind, arg, dst=None):
            if dst is None:
                i = slot_ctr[0] % NSLOT
                slot_ctr[0] += 1
                t, r = wslot[i]
            else:
                t, r = dst
            if kind == "plain":
                src, ncol = arg
                em.dma("pool", t[:, :, 0:ncol], src, writes=[r])
            elif kind == "perm":
                cb, h0 = arg
                dstv = t[:].rearrange("p kc (hg kv d) -> p kc hg kv d", kv=2, d=64)
                for kv in range(2):
                    c0 = cb + kv * 512 + h0 * 64
                    for hgi in range(4):
                        em.dma("pool", dstv[:, :, hgi, kv, :], winv[:, :, c0 + hgi * 64:c0 + (hgi + 1) * 64],
                               writes=[r], part=(kv + hgi > 0))
            elif kind == "rowperm":
                c0 = arg
                for kv in range(2):
                    src = wao_d[kv * 512:(kv + 1) * 512, c0:c0 + 512].rearrange("(hg d) n -> d hg n", d=64)
                    em.dma("pool", t[kv * 64:(kv + 1) * 64, :, :], src, writes=[r], part=(kv > 0))
            return t, r

        def rstd_chain(s_t, s_r, in_ap, tmp_ap, out_ap, scale, after_accum=False):
            if after_accum:
                em.op("act", lambda e: e.copy(out=s_t[:, 7:8], in_=epsc[:, 0:1]), reads=[r_epsc], writes=[s_r])
            em.op("act", lambda e: e.activation(out=tmp_ap, in_=in_ap, func=AF.Ln, bias=epsc[:, 0:1], scale=scale),
                  reads=[s_r, r_epsc], writes=[s_r])
            em.op("act", lambda e: e.activation(out=out_ap, in_=tmp_ap, func=AF.Exp, scale=-0.5), reads=[s_r], writes=[s_r])

        xload_ctr = [0]
        stat_ctr = [0]

        def stage_A_batch(items):
            n = len(items)
            s_t, s_r = stt[stat_ctr[0] % 4]
            stat_ctr[0] += 1
            bufs = []
            for i, (src_ap, dst_ap, dst_r) in enumerate(items):
                k = xload_ctr[0] % 4
                xload_ctr[0] += 1
                x_t, x_r = xt[k]
                n_t, n_r = xn[i % 2]
                bufs.append((x_t, x_r, n_t, n_r))
                em.dma("sp", x_t[:], src_ap, writes=[x_r])
                em.op("act", lambda e: e.activation(out=sqj[:], in_=x_t[:], func=AF.Square, accum_out=s_t[:, i:i + 1]),
                      reads=[x_r], writes=[r_sqj, s_r])
            rstd_chain(s_t, s_r, s_t[:, 0:n], s_t[:, 2:2 + n], s_t[:, 4:4 + n], 1.0 / D, after_accum=True)
            for i, (x_t, x_r, n_t, n_r) in enumerate(bufs):
                em.op("dve", lambda e: e.scalar_tensor_tensor(out=n_t[:], in0=x_t[:], scalar=s_t[:, 4 + i:5 + i], in1=gbc[:],
                                                              op0=ALU.mult, op1=ALU.mult),
                      reads=[x_r, s_r, r_gbc], writes=[n_r])
            return bufs

        def stage_A_batch_p2(items, bufs):
            for i, (x_t, x_r, n_t, n_r) in enumerate(bufs):
                pT = pT7 if i == 0 else pT6
                for kc in range(8):
                    f = em.op if kc == 7 else em.op_noinc
                    f("pe", lambda e, kc=kc: e.transpose(out=pT[:, kc * 128:(kc + 1) * 128], in_=n_t[:, kc * 128:(kc + 1) * 128],
                                                         identity=identb[:]),
                      reads=[n_r, r_identb], writes=[r_bank[7 - i]])
            for i, (src_ap, dst_ap, dst_r) in enumerate(items):
                pT = pT7 if i == 0 else pT6
                em.op("act", lambda e: e.copy(out=dst_ap, in_=pT.rearrange("p (k t) -> p k t", k=8)),
                      reads=[r_bank[7 - i]], writes=[dst_r])

        def B_mm(t, xT_ap_fn, w_q0, w_q1, w_kv, halo=False):
            b0, b1, b2 = 0, 1, 2
            for kc in range(8):
                last = kc == 7
                if not halo:
                    em.op_noinc("pe", lambda e, kc=kc: e.matmul(psA[:, b0, :].rearrange("p (a b c) -> p a b c", a=4, b=2), lhsT=xT_ap_fn(kc, 0, 128), rhs=permview(w_q0[0], kc)[:, 0:4, :, :],
                                                                start=(kc == 0), stop=(kc == 7)),
                                reads=[xT_ap_fn.res, w_q0[1]], writes=[r_bank[b0]])
                    em.op_noinc("pe", lambda e, kc=kc: e.matmul(psA[:, b1, :].rearrange("p (a b c) -> p a b c", a=4, b=2), lhsT=xT_ap_fn(kc, 0, 128), rhs=permview(w_q1[0], kc)[:, 4:8, :, :],
                                                                start=(kc == 0), stop=(kc == 7)),
                                reads=[xT_ap_fn.res, w_q1[1]], writes=[r_bank[b1]])
                f = em.op if last else em.op_noinc
                f("pe", lambda e, kc=kc: e.matmul(psA[:, b2, 0:256], lhsT=xT_ap_fn(kc, 0, 128), rhs=w_kv[0][:, kc, 0:256],
                                                  start=(kc == 0), stop=(kc == 7)),
                  reads=[xT_ap_fn.res, w_kv[1]], writes=[r_bank[0], r_bank[1], r_bank[2]] if last else [r_bank[b2]])

        def B_evac(t, par, qT_dst, halo=False):
            b2 = 2
            q_t, q_r = qb[0]
            r_t, r_r = rt[0]
            rbanks = [r_bank[2]] if halo else [r_bank[0], r_bank[1], r_bank[2]]
            if halo:
                nh = 2
                ps_ap = psA[:, b2, 0:128].rearrange("p (h d) -> p h d", d=64)
                o_ap = q_t[:, 1024:1152].rearrange("p (h d) -> p h d", d=64)
            else:
                nh = 18
                ps_ap = psA[:, 0:3, :].rearrange("p b c -> p (b c)")[:, 0:1152].rearrange("p (h d) -> p h d", d=64)
                o_ap = q_t[:, 0:1152].rearrange("p (h d) -> p h d", d=64)
            r_ph = em.res("phase")
            va_out = VA[:, t % NVR, :, :].rearrange("p g c -> p (g c)")
            em.op("act", lambda e: e.copy(out=o_ap[:, :, 16:64], in_=ps_ap[:, :, 16:64]), reads=rbanks, writes=[q_r, r_ph])
            if not halo:
                for g in range(2):
                    em.op("act", lambda e, g=g: e.copy(out=va_out[:, g * 192:g * 192 + 64], in_=psA[:, b2, 128 + g * 64:192 + g * 64]),
                          reads=[r_bank[b2]], writes=[r_VA[t % NVR], r_ph])
            cosb = cs[:, t, 0:8].unsqueeze(1).to_broadcast([128, nh, 8])
            sinb = cs[:, t, 8:16].unsqueeze(1).to_broadcast([128, nh, 8])
            x1 = ps_ap[:, :, 0:8]
            x2 = ps_ap[:, :, 8:16]
            T = [r_t[:, j, 0:nh * 8].rearrange("p (h d) -> p h d", d=8) for j in range(4)]
            em.op("dve", lambda e: e.tensor_tensor(out=T[0], in0=x1, in1=cosb, op=ALU.mult), reads=rbanks + [r_cs, r_ph], writes=[r_r])
            em.op("dve", lambda e: e.tensor_tensor(out=T[1], in0=x2, in1=sinb, op=ALU.mult), reads=rbanks + [r_cs], writes=[r_r])
            em.op("dve", lambda e: e.tensor_tensor(out=T[2], in0=x2, in1=cosb, op=ALU.mult), reads=rbanks + [r_cs], writes=[r_r])
            em.op("dve", lambda e: e.tensor_tensor(out=T[3], in0=x1, in1=sinb, op=ALU.mult), reads=rbanks + [r_cs], writes=[r_r])
            em.op("dve", lambda e: e.tensor_tensor(out=o_ap[:, :, 0:8], in0=T[0], in1=T[1], op=ALU.subtract), reads=[r_r], writes=[q_r])
            em.op("dve", lambda e: e.tensor_tensor(out=o_ap[:, :, 8:16], in0=T[2], in1=T[3], op=ALU.add), reads=[r_r], writes=[q_r])
            if halo:
                for g in range(2):
                    em.op("dve", lambda e, g=g: e.tensor_scalar(out=va_out[:, g * 192:g * 192 + 64], in0=psA[:, b2, 128 + g * 64:192 + g * 64],
                                                                scalar1=validt[:, 0:1], scalar2=None, op0=ALU.mult),
                          reads=[r_bank[b2], r_valid], writes=[r_VA[t % NVR]])
        def B_evac_b(t, par, qT_dst, halo=False):
            q_t, q_r = qb[0]
            if not halo:
                for hg in range(8):
                    f = em.op if hg == 7 else em.op_noinc
                    f("pe", lambda e, hg=hg: e.transpose(out=pT6[:, hg * 128:(hg + 1) * 128], in_=q_t[:, hg * 128:(hg + 1) * 128],
                                                         identity=identb[:]),
                      reads=[q_r, r_identb], writes=[r_bank[6]])
                em.op("dve", lambda e: e.tensor_copy(out=qT_dst[0], in_=pT6.rearrange("p (k t) -> p k t", k=8)),
                      reads=[r_bank[6]], writes=[qT_dst[1]])
            em.op("pe", lambda e: e.transpose(out=pT7[:, 0:128], in_=q_t[:, 1024:1152], identity=identb[:]),
                  reads=[q_r, r_identb], writes=[r_bank[7]])
            for g in range(2):
                em.op("dve", lambda e: e.tensor_copy(out=kTp[g * 64:(g + 1) * 64, t % NVR, g, :], in_=pT7[g * 64:(g + 1) * 64, 0:128]),
                      reads=[r_bank[7]], writes=[r_kT[t % NVR]])

        def feat_tile(w, col, rhs_fn, rhs_res, nk=8, bank=None, lhs_fn=None):
            b = fb() if bank is None else bank
            for k in range(nk):
                f = em.op if k == nk - 1 else em.op_noinc
                f("pe", lambda e, k=k: e.matmul(psA[:, b, :], lhsT=(w[0][:, k, col:col + 128] if lhs_fn is None else lhs_fn(k)), rhs=rhs_fn(k),
                                                start=(k == 0), stop=(k == nk - 1)),
                  reads=[w[1]] + rhs_res, writes=[r_bank[b]])
            return b

        def do_stage_A(j):
            prev_stage = em.stage
            em.stage = f"{j}A"
            xT_t, xT_r = xnT[j % 2]
            if j == 0:
                it = [(xh_d, xnTh[:], r_xnTh)]
                stage_A_batch_p2(it, stage_A_batch(it))
            for s0 in (0, 2):
                it = A_items(j, s0)
                stage_A_batch_p2(it, stage_A_batch(it))
            em.stage = prev_stage

        def A_items(j, s0):
            xT_t, xT_r = xnT[j % 2]
            return [(x_d[(4 * j + s) * 128:(4 * j + s + 1) * 128, :], xT_t[:, :, s * 128:(s + 1) * 128], xT_r) for s in (s0, s0 + 1)]

        def A_part(j, s0, part, state):
            prev_stage = em.stage
            em.stage = f"{j}A"
            if part == 1:
                state[s0] = stage_A_batch(A_items(j, s0))
            else:
                stage_A_batch_p2(A_items(j, s0), state[s0])
            em.stage = prev_stage

        w_ga_cur = [None]

        def stage_B_sub(j, s):
            xT_t, xT_r = xnT[j % 2]
            prev = em.stage
            em.stage = f"{j}B"
            t = 4 * j + s + 1

            def xts(kc, lo, hi):
                return xT_t[:, kc, s * 128 + lo:s * 128 + hi]
            xts.res = xT_r
            B_mm(t, xts, wres["q"], wres["q"], wres["kv"])
            B_evac(t, s % 2, None)
            em.stage = f"{j}D"
            for hg in (2 * s, 2 * s + 1):
                bk = 3 + hg % 3
                feat_tile(wres["za0" if hg < 4 else "za1"], (hg % 4) * 128, lambda k: xT_t[:, k, :], [xT_r], bank=bk)
                em.op("act", lambda e: e.activation(out=A1[:, hg, :], in_=psA[:, bk, :], func=AF.Silu),
                      reads=[r_bank[bk]], writes=[r_A1])
            for dt in (2 * s, 2 * s + 1):
                if dt % 4 == 0:
                    w_ga_cur[0] = load_unit("plain", (winv[:, :, 5376 + (dt // 4) * 512:5376 + (dt // 4 + 1) * 512], 512))
                bk = 3 + (dt + 2) % 3
                feat_tile(w_ga_cur[0], (dt % 4) * 128, lambda k: xT_t[:, k, :], [xT_r], bank=bk)
                em.op("act", lambda e: e.activation(out=gaT[:, dt, :], in_=psA[:, bk, :], func=AF.Tanh, bias=bmh[:, dt:dt + 1], scale=0.5),
                      reads=[r_bank[bk], r_bmh], writes=[r_gaT])
            em.stage = f"{j}B"
            B_evac_b(t, s % 2, (A2[:, :, s * 128:(s + 1) * 128], r_A2))
            em.stage = prev

        class _Stop(Exception):
            pass

        _cnt = {}

        def chk(name):
            _cnt[name] = _cnt.get(name, 0) + 1
            if stop == name or stop == f"{name}#{_cnt[name]}":
                raise _Stop()

        try:
          chk("setup")
          do_stage_A(0)
          late_setup()
          chk("A")
          sctr = 0
          for j in range(NBLK):
              xT_t, xT_r = xnT[j % 2]
              sza_t = yaT_t = A1
              qT_t = sT_t = A2
              aT_t = vh_t = yT_t = A3
              if j == 0:
                  load_unit("plain", (winv[:, :, 1024:1280], 256), dst=wres["kv"])
                  load_unit("plain", (winv[:, :, 0:1024], 1024), dst=wres["q"])
                  load_unit("perm", (1280, 0), dst=wres["za0"])
                  load_unit("perm", (1280, 4), dst=wres["za1"])
              w_kv = wres["kv"]
              chk('Bw')
              em.stage = f"{j}B"
              if j == 0:
                  def xth(kc, lo, hi):
                      return xnTh[:, kc, lo:hi]
                  xth.res = r_xnTh
                  B_mm(0, xth, None, None, w_kv, halo=True)
                  B_evac(0, 1, None, halo=True)
                  B_evac_b(0, 1, None, halo=True)

              for s in range(4):
                  if not (j > 0 and s == 0):
                      stage_B_sub(j, s)
              if j == 0:
                  dump("xnT", xT_t, xT_r)
                  dump("qT", A2, r_A2)
                  dump("sza", A1, r_A1)
              chk("B")
              chk("D")
              em.stage = f"{j}C"
              def cpar(c):
                  t = 4 * j + c // 2 + 1
                  if c % 2 == 0:
                      d = dict(s1=(t - 1) % NVR, s2=t % NVR, half=0, h1=(t - 1 == 0), h2=False)
                  else:
                      d = dict(s1=t % NVR, s2=(t - 1) % NVR, half=1, h1=False, h2=(t - 1 == 0))
                  q0 = (c // 2) * 128 + (c % 2) * 64
                  d["qsl"] = slice(q0, q0 + 64)
                  d["p1"] = PT1[c % 2]
                  d["p2"] = PT2[c % 2]
                  d["bPV"] = 4 + 2 * (c % 2)
                  d["bDN"] = 5 + 2 * (c % 2)
                  return d

              def emit_S(c, g):
                  d = cpar(c)
                  ps1, ps2 = 2 * g, 2 * g + 1
                  qsl = d["qsl"]
                  for ps, sl in ((ps1, d["s1"]), (ps2, d["s2"])):
                      em.op("pe", lambda e: e.matmul(psA[:, ps, :].rearrange("p (h q) -> p h q", q=64), lhsT=kTp[:, sl, g, :],
                                                     rhs=qT_t[:, :, qsl], start=True, stop=True),
                            reads=[r_kT[sl], r_A2], writes=[r_bank[ps]])

              def emit_exp(c, g):
                  d = cpar(c)
                  ps1, ps2 = 2 * g, 2 * g + 1
                  p1_t, p1_r = d["p1"]
                  p2_t, p2_r = d["p2"]
                  hs = slice(d["half"] * 64, d["half"] * 64 + 64)
                  em.op("act", lambda e: e.activation(out=p1_t[:, g, :], in_=psA[:, ps1, :], func=AF.Exp, scale=0.125),
                        reads=[r_bank[ps1]], writes=[p1_r])
                  em.op("act", lambda e: e.activation(out=p2_t[hs, g, :], in_=psA[hs, ps2, :], func=AF.Exp, scale=0.125),
                        reads=[r_bank[ps2]], writes=[p2_r])

              def emit_PV(c, g):
                  d = cpar(c)
                  p1_t, p1_r = d["p1"]
                  p2_t, p2_r = d["p2"]
                  bPV, bDN = d["bPV"], d["bDN"]
                  V1, V2, rV1, rV2 = VA[:, d["s1"]], VA[:, d["s2"]], r_VA[d["s1"]], r_VA[d["s2"]]
                  o1 = onesph if d["h1"] else onesp
                  o2 = onesph if d["h2"] else onesp
                  ro1 = r_onesph if d["h1"] else r_onesp
                  ro2 = r_onesph if d["h2"] else r_onesp
                  em.op("pe", lambda e: e.matmul(psA[:, bPV, :], lhsT=V1[:, g, :], rhs=p1_t[:, g, :], start=(g == 0), stop=False),
                        reads=[rV1, p1_r], writes=[r_bank[bPV]])
                  em.op("pe", lambda e: e.matmul(psA[:, bPV, :], lhsT=V2[:, g, :], rhs=p2_t[:, g, :], start=False, stop=(g == 1)),
                        reads=[rV2, p2_r], writes=[r_bank[bPV]])
                  em.op("pe", lambda e: e.matmul(psA[:, bDN, :], lhsT=o1[:, g, :], rhs=p1_t[:, g, :], start=(g == 0), stop=False),
                        reads=[ro1, p1_r], writes=[r_bank[bDN]])
                  em.op("pe", lambda e: e.matmul(psA[:, bDN, :], lhsT=o2[:, g, :], rhs=p2_t[:, g, :], start=False, stop=False),
                        reads=[ro2, p2_r], writes=[r_bank[bDN]])
                  if g == 1:
                      em.op("pe", lambda e: e.matmul(psA[:, bDN, :], lhsT=sel2[:], rhs=esrhs[:], start=False, stop=True),
                            reads=[r_sel2, r_esrhs], writes=[r_bank[bDN]])

              def emit_norm(c):
                  d = cpar(c)
                  bPV, bDN, qsl = d["bPV"], d["bDN"], d["qsl"]
                  dn_t, dn_r = f5["dn"][c % 2]
                  oo_t, oo_r = f5["oo"][c % 2]
                  HC = 384
                  em.op("act", lambda e: e.activation(out=dn_t[:, 0:HC], in_=psA[:, bDN, 0:HC], func=AF.Ln), reads=[r_bank[bDN]], writes=[dn_r])
                  em.op("act", lambda e: e.activation(out=dn_t[:, 0:HC], in_=dn_t[:, 0:HC], func=AF.Exp, scale=-1.0), reads=[dn_r], writes=[dn_r])
                  em.op("dve", lambda e: e.reciprocal(out=dn_t[:, HC:512], in_=psA[:, bDN, HC:512]), reads=[r_bank[bDN], dn_r], writes=[dn_r])
                  em.op("dve", lambda e: e.tensor_tensor(out=oo_t[:], in0=psA[:, bPV, :], in1=dn_t[:], op=ALU.mult),
                        reads=[r_bank[bPV], dn_r], writes=[oo_r])
                  em.op("dve", lambda e: e.tensor_tensor(out=A3[:, :, qsl], in0=oo_t[:].rearrange("p (h q) -> p h q", q=64),
                                                         in1=A1[:, :, qsl], op=ALU.mult),
                        reads=[oo_r, r_A1], writes=[r_A3])

              em.op("dve", lambda e: e.memset(PT2[0][0][64:128, :, :], 0.0), writes=[PT2[0][1]])
              em.op("dve", lambda e: e.memset(PT2[1][0][0:64, :, :], 0.0), writes=[PT2[1][1]])
              emit_S(0, 0)
              emit_S(0, 1)
              emit_exp(0, 0)
              emit_exp(0, 1)
              for c in range(8):
                  emit_PV(c, 0)
                  if c + 1 < 8:
                      emit_S(c + 1, 0)
                  emit_PV(c, 1)
                  if c + 1 < 8:
                      emit_S(c + 1, 1)
                      emit_exp(c + 1, 0)
                      emit_exp(c + 1, 1)
                  emit_norm(c)
              if j == 0:
                  dump("aT", A3, r_A3)
                  if debug:
                      em.dma("sp", dbg_d["VA"], VA[:], reads=r_VA)
                      dumped.extend(r_VA)
              chk("C")
              w_vs = [load_unit("plain", (winv[:, :, 3328 + u * 512:3328 + (u + 1) * 512], 512)) for u in range(2)]
              f1_state = {}

              def F1_batch(s0, part):
                  prev = em.stage
                  em.stage = f"{j}F"
                  if part == 1:
                      s_t, s_r = stt[stat_ctr[0] % 4]
                      stat_ctr[0] += 1
                      f1_state[s0] = (s_t, s_r)
                      banks = {}
                      for s in (s0, s0 + 1):
                          for u in range(2):
                              b = fb()
                              banks[(s, u)] = b
                              for kc in range(8):
                                  f = em.op if kc == 7 else em.op_noinc
                                  f("pe", lambda e, kc=kc: e.matmul(psA[:, b, :], lhsT=xT_t[:, kc, s * 128:(s + 1) * 128], rhs=w_vs[u][0][:, kc, :],
                                                                    start=(kc == 0), stop=(kc == 7)),
                                    reads=[xT_r, w_vs[u][1]], writes=[r_bank[b]])
                      for s in (s0, s0 + 1):
                          gv_t, gv_r = gv[s % 2]
                          for u in range(2):
                              b = banks[(s, u)]
                              em.op("act", lambda e: e.activation(out=gv_t[:, u * 512:(u + 1) * 512], in_=psA[:, b, :], func=AF.Gelu),
                                    reads=[r_bank[b]], writes=[gv_r])
                      for s in (s0, s0 + 1):
                          gv_t, gv_r = gv[s % 2]
                          bs_t, bs_r = bst[s % 2]
                          for u in range(2):
                              em.op("dve", lambda e: e.bn_stats(out=bs_t[:, u, :], in_=gv_t[:, u * 512:(u + 1) * 512]),
                                    reads=[gv_r], writes=[bs_r])
                          em.op("dve", lambda e: e.bn_aggr(out=s_t[:, 2 * (s % 2):2 * (s % 2) + 2], in_=bs_t[:].rearrange("p a b -> p (a b)")),
                                reads=[bs_r], writes=[s_r])
                      var_ap = s_t[:, 0:4].rearrange("p (a b) -> p a b", b=2)[:, :, 1]
                      rstd_chain(s_t, s_r, var_ap, s_t[:, 4:6], s_t[:, 6:8], 1.0)
                  else:
                      s_t, s_r = f1_state[s0]
                      for s in (s0, s0 + 1):
                          gv_t, gv_r = gv[s % 2]
                          vh_dst = A3[:, (2 * s):(2 * s + 2), :].rearrange("p a b -> p (a b)")
                          em.op("dve", lambda e: e.tensor_scalar(out=vh_dst, in0=gv_t[:], scalar1=s_t[:, 2 * (s % 2):2 * (s % 2) + 1],
                                                                 scalar2=s_t[:, 6 + s % 2:7 + s % 2], op0=ALU.subtract, op1=ALU.mult),
                                reads=[gv_r, s_r], writes=[r_A3])
                  em.stage = prev

              F1_batch(0, 1)
              em.stage = f"{j}E"
              for u in range(2):
                  w_ao = load_unit("rowperm", u * 512)
                  for dd in range(4):
                      dt = u * 4 + dd
                      bx = feat_tile(w_ao, dd * 128, lambda k: A3[:, k, :], [r_A3])
                      em.op("dve", lambda e: e.scalar_tensor_tensor(out=A1[:, dt, :], in0=gaT[:, dt, :], scalar=1.0, in1=psA[:, bx, :],
                                                                    op0=ALU.add, op1=ALU.mult),
                            reads=[r_bank[bx], r_gaT], writes=[r_A1])
              if j == 0:
                  dump("yaT", A1, r_A1)
              chk("E")
              em.stage = f"{j}F"
              a_state = {}
              F1_batch(0, 2)
              F1_batch(2, 1)
              F1_batch(2, 2)
              w_u = [None, None]
              w_zs = [None, None]
              for g in range(8):
                  if j + 1 < NBLK:
                      if g == 0:
                          A_part(j + 1, 0, 1, a_state)
                      elif g == 2:
                          A_part(j + 1, 0, 2, a_state)
                      elif g == 3:
                          A_part(j + 1, 2, 1, a_state)
                      elif g == 5:
                          A_part(j + 1, 2, 2, a_state)
                  if g % 4 == 0:
                      w_u[g // 4] = load_unit("plain", (winv[:, :, 2304 + (g // 4) * 512:2304 + (g // 4 + 1) * 512], 512))
                      w_zs[g // 4] = load_unit("plain", (winv[:, :, 4352 + (g // 4) * 512:4352 + (g // 4 + 1) * 512], 512))
                  bu = feat_tile(w_u[g // 4], (g % 4) * 128, lambda k: xT_t[:, k, :], [xT_r])
                  gu_t, gu_r = f5["gu"][g % 2]
                  em.op("act", lambda e, bu=bu, gu_t=gu_t: e.activation(out=gu_t[:], in_=psA[:, bu, :], func=AF.Gelu), reads=[r_bank[bu]], writes=[gu_r])
                  bz = feat_tile(w_zs[g // 4], (g % 4) * 128, lambda k: xT_t[:, k, :], [xT_r])
                  sz_t, sz_r = f5["szs"][g % 2]
                  em.op("act", lambda e, bz=bz, sz_t=sz_t: e.activation(out=sz_t[:], in_=psA[:, bz, :], func=AF.Tanh, scale=0.5), reads=[r_bank[bz]], writes=[sz_r])
                  em.op("dve", lambda e, bz=bz, sz_t=sz_t: e.scalar_tensor_tensor(out=sz_t[:], in0=sz_t[:], scalar=1.0, in1=psA[:, bz, :],
                                                                              op0=ALU.add, op1=ALU.mult),
                        reads=[r_bank[bz], sz_r], writes=[sz_r])
                  bmx = fb()
                  for s in range(4):
                      em.op("pe", lambda e, s=s, g=g, bmx=bmx: e.matmul(
                          psA[:, bmx, s * 128:(s + 1) * 128],
                          lhsT=A3[:, (2 * s):(2 * s + 2), :].rearrange("p a b -> p (a b)")[:, g * 128:(g + 1) * 128],
                          rhs=wsTb[:, g, :], start=True, stop=True),
                          reads=[r_A3, r_wsTb], writes=[r_bank[bmx]])
                  m_t, m_r = f5["msb"][g % 2]
                  em.op("dve", lambda e, g=g, bmx=bmx, m_t=m_t: e.scalar_tensor_tensor(
                      out=m_t[:].rearrange("p (s i) -> p s i", i=128), in0=psA[:, bmx, :].rearrange("p (s i) -> p s i", i=128), scalar=lng[:, g:g + 1],
                      in1=Cc[:, g, :].unsqueeze(1).to_broadcast([128, 4, 128]), op0=ALU.mult, op1=ALU.add),
                      reads=[r_bank[bmx], r_lng, r_Cc], writes=[m_r])
                  em.op("dve", lambda e, m_t=m_t, gu_t=gu_t: e.scalar_tensor_tensor(out=m_t[:], in0=m_t[:], scalar=0.5, in1=gu_t[:], op0=ALU.mult, op1=ALU.mult),
                        reads=[m_r, gu_r], writes=[m_r])
                  em.op("dve", lambda e, g=g, m_t=m_t, sz_t=sz_t: e.tensor_tensor(out=A2[:, g, :], in0=m_t[:], in1=sz_t[:], op=ALU.mult),
                        reads=[m_r, sz_r], writes=[r_A2])
              if j == 0:
                  dump("sT", A2, r_A2)
              chk("F")
              em.stage = f"{j}G"
              h_x = []
              for s in range(4):
                  k = xload_ctr[0] % 4
                  xload_ctr[0] += 1
                  x_t, x_r = xt[k]
                  em.dma("sp", x_t[:], x_d[(4 * j + s) * 128:(4 * j + s + 1) * 128, :], writes=[x_r])
                  h_x.append((x_t, x_r))
              for u in range(2):
                  w_so = load_unit("plain", (wsov[:, :, u * 512:(u + 1) * 512], 512))
                  w_gs = load_unit("plain", (winv[:, :, 6400 + u * 512:6400 + (u + 1) * 512], 512))
                  for dd in range(4):
                      dt = u * 4 + dd
                      bx = feat_tile(w_so, dd * 128, lambda k: A2[:, k, :], [r_A2])
                      by = feat_tile(w_gs, dd * 128, lambda k: xT_t[:, k, :], [xT_r])
                      g_t, g_r = f5["gg"][dt % 2]
                      em.op("act", lambda e, by=by, dt=dt, g_t=g_t: e.activation(out=g_t[:], in_=psA[:, by, :], func=AF.Tanh, bias=bmh[:, 8 + dt:9 + dt], scale=0.5),
                            reads=[r_bank[by], r_bmh], writes=[g_r])
                      em.op("dve", lambda e, bx=bx, g_t=g_t: e.scalar_tensor_tensor(out=g_t[:], in0=g_t[:], scalar=1.0, in1=psA[:, bx, :],
                                                                                 op0=ALU.add, op1=ALU.mult),
                            reads=[r_bank[bx], g_r], writes=[g_r])
                      em.op("dve", lambda e, dt=dt, g_t=g_t: e.tensor_tensor(out=A3[:, dt, :], in0=g_t[:], in1=A1[:, dt, :], op=ALU.add),
                            reads=[g_r, r_A1], writes=[r_A3])
              if j == 0:
                  dump("yT", A3, r_A3)
              chk("G")
              em.stage = f"{j}H"
              w_oo = [load_unit("plain", (wov[:, :, u * 512:(u + 1) * 512], 512)) for u in range(2)]
              for s0 in (0, 2):
                  s_t, s_r = stt[stat_ctr[0] % 4]
                  stat_ctr[0] += 1
                  tiles = []
                  for s in (s0, s0 + 1):
                      t = 4 * j + s
                      x_t, x_r = h_x[s]
                      bb = []
                      for u in range(2):
                          b = fb()
                          bb.append(b)
                          for dt in range(8):
                              f = em.op if dt == 7 else em.op_noinc
                              f("pe", lambda e, dt=dt: e.matmul(psA[:, b, :], lhsT=A3[:, dt, s * 128:(s + 1) * 128], rhs=w_oo[u][0][:, dt, :],
                                                                start=(dt == 0), stop=(dt == 7)),
                                reads=[r_A3, w_oo[u][1]], writes=[r_bank[b]])
                      tiles.append((t, x_t, x_r, bb))
                  for (t, x_t, x_r, bb) in tiles:
                      for u in range(2):
                          em.op("dve", lambda e: e.scalar_tensor_tensor(out=x_t[:, u * 512:(u + 1) * 512], in0=psA[:, bb[u], :], scalar=0.5,
                                                                        in1=x_t[:, u * 512:(u + 1) * 512], op0=ALU.mult, op1=ALU.add),
                                reads=[r_bank[bb[u]]], writes=[x_r])
                  for i, (t, x_t, x_r, bb) in enumerate(tiles):
                      em.op("act", lambda e: e.activation(out=sqj[:], in_=x_t[:], func=AF.Square, accum_out=s_t[:, i:i + 1]),
                            reads=[x_r], writes=[r_sqj, s_r])
                  rstd_chain(s_t, s_r, s_t[:, 0:2], s_t[:, 2:4], s_t[:, 4:6], 1.0 / D, after_accum=True)
                  for i, (t, x_t, x_r, bb) in enumerate(tiles):
                      em.op("dve", lambda e: e.scalar_tensor_tensor(out=x_t[:], in0=x_t[:], scalar=s_t[:, 4 + i:5 + i], in1=fgbc[:],
                                                                    op0=ALU.mult, op1=ALU.mult),
                            reads=[x_r, s_r, r_fgbc], writes=[x_r])
                      em.dma("sp", out_d[t * 128:(t + 1) * 128, :], x_t[:], reads=[x_r])
                  if s0 == 0 and j + 1 < NBLK:
                      stage_B_sub(j + 1, 0)
        except _Stop:
            pass
        em.wait_all("sp", dumped + [r for (_, r) in wslot] + [r for (_, r) in xt])
        em.finish()
    return nc


def rope_table(pos):
    half = 8
    inv_freq = (np.float32(500000.0) ** (-(np.arange(half, dtype=np.float32) * np.float32(2.0)) / np.float32(16))).astype(np.float32)
    ang = (pos.astype(np.float32)[:, None] * inv_freq[None, :]).astype(np.float32)
    return np.concatenate([np.cos(ang), np.sin(ang)], axis=1).astype(np.float32)


def core_inputs(x_core, x_halo, valid, pos0, params):
    ntok = x_core.shape[0]
    nt = ntok // 128
    cs = rope_table(pos0 + np.arange((nt + 1) * 128)).reshape(nt + 1, 128, 16).transpose(1, 0, 2)
    m = dict(params)
    m["x"] = np.ascontiguousarray(x_core, dtype=np.float32)
    m["xh"] = np.ascontiguousarray(x_halo, dtype=np.float32)
    m["cs"] = np.ascontiguousarray(cs, dtype=np.float32)
    m["valid"] = np.full((128, 1), valid, np.float32)
    return m


def shared_params(norm_g, w_in, b_merge, att_sinks, sg_w, sg_b, sg_ln_g, sg_ln_b, w_att_out, w_sg_out, w_o, final_g):
    f = lambda a: np.ascontiguousarray(np.asarray(a), dtype=np.float32)
    sk = np.asarray(att_sinks)[0]
    p = {
        "ident": np.eye(128, dtype=np.float32),
        "w_in": f(w_in[0]), "w_att_out": f(w_att_out[0]), "w_sg_out": f(w_sg_out[0]), "w_o": f(w_o[0]),
        "norm_g": f(norm_g[0]).reshape(1, D), "final_g": f(final_g).reshape(1, D),
        "bm": f(np.asarray(b_merge)[0].reshape(16, 128).T),
        "lng": f(np.asarray(sg_ln_g)[0].reshape(8, 128).T), "lnb": f(np.asarray(sg_ln_b)[0].reshape(8, 128).T),
        "sinks": f(np.tile(sk[None, :], (128, 1))),
        "wsT": f(np.asarray(sg_w)[0].transpose(2, 0, 1)),
        "sgb": f(np.asarray(sg_b)[0].reshape(1, 1024)),
    }
    return p


_NC_CACHE = {}


def kernel(x, norm_g, w_in, b_merge, att_sinks, sg_w, sg_b, sg_ln_g, sg_ln_b, w_att_out, w_sg_out, w_o, final_g):
    x = np.asarray(x)
    B, S, _ = x.shape
    params = shared_params(norm_g, w_in, b_merge, att_sinks, sg_w, sg_b, sg_ln_g, sg_ln_b, w_att_out, w_sg_out, w_o, final_g)
    per_b = NCORES // B
    ntok = S // per_b
    in_maps = []
    for c in range(NCORES):
        b, q = c // per_b, c % per_b
        s0 = q * ntok
        halo = x[b, s0 - 128:s0] if q > 0 else np.zeros((128, D), np.float32)
        in_maps.append(core_inputs(x[b, s0:s0 + ntok], halo, 1.0 if q > 0 else 0.0, s0 - 128, params))
    if "nc" not in _NC_CACHE:
        _NC_CACHE["nc"] = build(NBLK=ntok // 512)
    res = run_bass_kernel_spmd(_NC_CACHE["nc"], in_maps, core_ids=list(range(NCORES)))
    out = np.empty((B, S, D), np.float32)
    for c in range(NCORES):
        b, q = c // per_b, c % per_b
        out[b, q * ntok:(q + 1) * ntok] = res.results[c]["out"]
    return out
```

```python
import numpy as np
from contextlib import ExitStack
import concourse.bass as bass
import concourse.mybir as mybir
from concourse.bass_utils import run_bass_kernel_spmd

F32 = mybir.dt.float32
BF16 = mybir.dt.bfloat16
AF = mybir.ActivationFunctionType
ALU = mybir.AluOpType

D = 1024
NCORES = 8
TOK_PER_CORE = 2048
EPS = 1e-6
NSLOT = 5
NVR = 6


class Res:
    __slots__ = ("name", "w", "r", "dsem", "dcnt")

    def __init__(self, name):
        self.name = name
        self.w = []
        self.r = []
        self.dsem = None
        self.dcnt = 0


class _Rec:
    def __init__(self):
        self.call = None

    def __getattr__(self, name):
        def f(*a, **k):
            self.call = (name, a, k)
            return self
        return f


def _bind(fn):
    rec = _Rec()
    fn(rec)
    name, a, k = rec.call
    return lambda e: getattr(e, name)(*a, **k)


class Emitter:
    ENGS = ("pe", "act", "dve", "pool", "sp")

    def __init__(self, nc, stack):
        self.nc = nc
        self.stack = stack
        self.prog = {e: [] for e in self.ENGS}
        self.esem = {}
        self.ecnt = {e: 0 for e in self.ENGS}
        self.waited = {e: {} for e in self.ENGS}
        self.sems = {}
        for e in self.ENGS:
            s = stack.enter_context(nc.semaphore("s_" + e))
            self.esem[e] = s
            self.sems[("e", e)] = s
        self.nres = 0
        self.stage = "setup"
        self.labels = {e: [] for e in self.ENGS}

    def res(self, name=None):
        self.nres += 1
        return Res(f"{name or 'r'}_{self.nres}")

    def _dsem(self, r):
        if r.dsem is None:
            s = self.stack.enter_context(self.nc.semaphore("d_" + r.name))
            r.dsem = ("d", r.name)
            self.sems[r.dsem] = s
        return r.dsem

    def _wait(self, eng, toks):
        wd = self.waited[eng]
        need = {}
        own = ("e", eng)
        for k, v in toks:
            if k == own and (eng == "pe" or v > self.ecnt[eng]):
                continue
            if wd.get(k, 0) < v and need.get(k, 0) < v:
                need[k] = v
        for k, v in need.items():
            wd[k] = v
            sem = self.sems[k]
            self.prog[eng].append(lambda e, sem=sem, v=v: e.wait_ge(sem, v))

    def _deps(self, reads, writes, eng):
        toks = []
        for r in reads:
            toks += r.w
        for w in writes:
            toks += w.w
            toks += [t for t in w.r if t[0] != ("e", eng)]
        return toks

    def op(self, eng, fn, reads=(), writes=()):
        self._wait(eng, self._deps(reads, writes, eng))
        self.ecnt[eng] += 1
        tok = (("e", eng), self.ecnt[eng])
        sem = self.esem[eng]
        fn = _bind(fn)
        self.labels[eng].append(self.stage)
        self.prog[eng].append(lambda e, fn=fn, sem=sem: fn(e).then_inc(sem, 1))
        for r in reads:
            r.r.append(tok)
        for w in writes:
            w.w = [tok]
            w.r = []
        return tok

    def op_noinc(self, eng, fn, reads=(), writes=()):
        self._wait(eng, self._deps(reads, writes, eng))
        fn = _bind(fn)
        self.labels[eng].append(self.stage)
        self.prog[eng].append(lambda e, fn=fn: fn(e))
        tok = (("e", eng), self.ecnt[eng] + 1)
        for r in reads:
            r.r.append(tok)
        for w in writes:
            w.w = [tok]
            w.r = []

    def dma(self, eng, out, in_, reads=(), writes=(), part=False):
        self._wait(eng, self._deps(reads, () if part else writes, eng))
        sr = writes[0] if writes else reads[0]
        k = self._dsem(sr)
        sr.dcnt += 16
        tok = (k, sr.dcnt)
        sem = self.sems[k]
        self.prog[eng].append(
            lambda e, out=out, in_=in_, sem=sem: e.dma_start(out=out, in_=in_).then_inc(sem, 16))
        for r in reads:
            r.r.append(tok)
        for w in writes:
            w.w = [tok]
            w.r = []
        return tok

    def wait_all(self, eng, ress):
        toks = []
        for r in ress:
            toks += r.w + r.r
        self._wait(eng, toks)

    def finish(self):
        with self.nc.Block() as block:
            def run(name):
                def f(e):
                    for c in self.prog[name]:
                        c(e)
                return f
            block.tensor(run("pe"))
            block.scalar(run("act"))
            block.vector(run("dve"))
            block.gpsimd(run("pool"))
            block.sync(run("sp"))


def build(NBLK=4, debug=False, stop=None):
    NT = NBLK * 4
    nc = bass.Bass("TRN2", target_bir_lowering=False)

    def din(name, shape):
        return nc.dram_tensor(name, shape, F32, kind="ExternalInput").ap()

    x_d = din("x", [NT * 128, D])
    xh_d = din("xh", [128, D])
    cs_d = din("cs", [128, NT + 1, 16])
    valid_d = din("valid", [128, 1])
    ident_d = din("ident", [128, 128])
    win_d = din("w_in", [D, 7424])
    wao_d = din("w_att_out", [D, D])
    wso_d = din("w_sg_out", [D, D])
    wo_d = din("w_o", [D, D])
    ng_d = din("norm_g", [1, D])
    fg_d = din("final_g", [1, D])
    bm_d = din("bm", [128, 16])
    lng_d = din("lng", [128, 8])
    lnb_d = din("lnb", [128, 8])
    snk_d = din("sinks", [128, 16])
    wsT_d = din("wsT", [128, 8, 128])
    sgb_d = din("sgb", [1, 1024])
    out_d = nc.dram_tensor("out", [NT * 128, D], F32, kind="ExternalOutput").ap()
    dbg_d = {}

    with ExitStack() as st:
        em = Emitter(nc, st)

        def sb(name, shape, dt=F32):
            return st.enter_context(nc.sbuf_tensor("sb_" + name, shape, dt)), em.res(name)

        identb, r_identb = sb("identb", [128, 128], BF16)
        gbc, r_gbc = sb("gbc", [128, D])
        fgbc, r_fgbc = sb("fgbc", [128, D])
        bm, r_bm = sb("bm", [128, 16])
        lng, r_lng = sb("lng", [128, 8])
        lnb, r_lnb = sb("lnb", [128, 8])
        snk, r_snk = sb("snk", [128, 16])
        esk, r_esk = sb("esk", [128, 16])
        esrhs, r_esrhs = sb("esrhs", [128, 512], BF16)
        sel2, r_sel2 = sb("sel2", [128, 128], BF16)
        wsTb, r_wsTb = sb("wsTb", [128, 8, 128], BF16)
        Cc, r_Cc = sb("Cc", [128, 8, 128])
        cs, r_cs = sb("cs", [128, NT + 1, 16])
        validt, r_valid = sb("valid", [128, 1])
        epsc, r_epsc = sb("epsc", [128, 1])
        onesb, r_onesb = sb("onesb", [128, 128], BF16)
        onesp, r_onesp = sb("onesp", [128, 2, 128], BF16)
        onesph, r_onesph = sb("onesph", [128, 2, 128], BF16)

        kTp, r_kTall = sb("kTp", [128, NVR, 2, 128], BF16)
        r_kT = [em.res("kT") for _ in range(NVR)]
        VA, r_VAall = sb("VA", [128, NVR, 2, 128], BF16)
        r_VA = [em.res("VA") for _ in range(NVR)]
        xnT = []
        for i in range(2):
            xnT.append(sb(f"xnT{i}", [128, 8, 512], BF16))
        xnTh, r_xnTh = sb("xnTh", [128, 8, 128], BF16)
        xt = [sb(f"xt{i}", [128, D]) for i in range(4)]
        xn = [sb(f"xn{i}", [128, D], BF16) for i in range(2)]
        sqj, r_sqj = sb("sqj", [128, D], BF16)
        stt = [sb(f"stt{i}", [128, 8]) for i in range(4)]
        A1, r_A1 = sb("A1", [128, 8, 512], BF16)
        A2, r_A2 = sb("A2", [128, 8, 512], BF16)
        gaT, r_gaT = sb("gaT", [128, 8, 512], BF16)
        vh0, r_vh0 = sb("vh0", [128, 2, 1024], BF16)
        bmh, r_bmh = sb("bmh", [128, 16])
        A3, r_A3 = sb("A3", [128, 8, 512], BF16)
        qb = [sb(f"qb{i}", [128, 1280], BF16) for i in range(1)]
        rt = [sb(f"rt{i}", [128, 4, 144]) for i in range(1)]
        gv = [sb(f"gv{i}", [128, D]) for i in range(2)]
        bst = [sb(f"bst{i}", [128, 2, 6]) for i in range(2)]
        f5 = {k: [sb(f"{k}{i}", [128, 512]) for i in range(2)] for k in ("msb", "gu", "szs", "gg")}
        f5["dn"] = f5["msb"]
        f5["oo"] = f5["gu"]
        PT1 = [(gv[i][0][:, 0:512].bitcast(BF16).rearrange("p (g c) -> p g c", g=2), gv[i][1]) for i in range(2)]
        PT2 = [(gv[i][0][:, 512:1024].bitcast(BF16).rearrange("p (g c) -> p g c", g=2), gv[i][1]) for i in range(2)]
        identf, r_identf = gv[1][0][:, 0:128], gv[1][1]
        wsTf, r_wsTf = gv[0][0][:, :].rearrange("p (g i) -> p g i", i=128), gv[0][1]
        wslot = [sb(f"ws{i}", [128, 8, 512], BF16) for i in range(NSLOT)]
        wres = {k: sb("wr_" + k, [128, 8, n], BF16) for k, n in (("q", 1024), ("kv", 256), ("za0", 512), ("za1", 512))}

        def permview(t, kc):
            return t[:, kc, :].rearrange("p (kv hg d) -> p hg kv d", kv=2, hg=8)

        psA = st.enter_context(nc.psum_tensor("psA", [128, 8, 512], F32))
        pT6 = psA[:, 6, :].bitcast(BF16)
        pT7 = psA[:, 7, :].bitcast(BF16)
        r_bank = [em.res(f"bank{i}") for i in range(8)]
        fbc = [0]

        def fb():
            b = fbc[0] % 6
            fbc[0] += 1
            return b

        if debug:
            for nm, shp, dt in (("xnT", [128, 8, 512], BF16), ("qT", [128, 8, 512], BF16),
                                ("sza", [128, 8, 512], BF16), ("aT", [128, 8, 512], BF16), ("yaT", [128, 8, 512], BF16),
                                ("sT", [128, 8, 512], BF16), ("yT", [128, 8, 512], BF16), ("VA", [128, NVR, 2, 128], BF16)):
                dbg_d[nm] = nc.dram_tensor("dbg_" + nm, shp, dt, kind="ExternalOutput").ap()
        r_dbg = em.res("dbg")

        dumped = []

        def dump(nm, t, r):
            if debug:
                em.dma("sp", dbg_d[nm], t[:], reads=[r])
                dumped.append(r)

        em.dma("sp", identf, ident_d, writes=[r_identf])
        em.dma("sp", gbc[:], ng_d.partition_broadcast(128), writes=[r_gbc])
        em.op("dve", lambda e: e.tensor_copy(out=identb[:], in_=identf), reads=[r_identf], writes=[r_identb])
        em.op("dve", lambda e: e.memset(epsc[:], EPS), writes=[r_epsc])
        def late_setup():
            em.dma("sp", wsTf, wsT_d, writes=[r_wsTf])
            for (t, r, d) in ((cs, r_cs, cs_d), (validt, r_valid, valid_d), (bm, r_bm, bm_d), (lng, r_lng, lng_d), (lnb, r_lnb, lnb_d),
                              (snk, r_snk, snk_d)):
                em.dma("sp", t[:], d, writes=[r])
            em.dma("sp", fgbc[:], fg_d.partition_broadcast(128), writes=[r_fgbc])
            bsbc = xt[3][0]
            em.dma("sp", bsbc[:], sgb_d.partition_broadcast(128), writes=[xt[3][1]])
            em.op("dve", lambda e: e.memset(onesb[:], 1.0), writes=[r_onesb])
            em.op("dve", lambda e: e.tensor_scalar(out=bmh[:], in0=bm[:], scalar1=0.5, scalar2=None, op0=ALU.mult), reads=[r_bm], writes=[r_bmh])
            em.op("dve", lambda e: e.memset(onesp[:], 0.0), writes=[r_onesp])
            em.op("dve", lambda e: e.memset(onesp[:, 0, 0:64], 1.0), writes=[r_onesp])
            em.op("dve", lambda e: e.memset(onesp[:, 1, 64:128], 1.0), writes=[r_onesp])
            em.op("dve", lambda e: e.tensor_scalar(out=onesph[:], in0=onesp[:], scalar1=validt[:, 0:1], scalar2=None, op0=ALU.mult),
                  reads=[r_onesp, r_valid], writes=[r_onesph])
            em.op("dve", lambda e: e.memset(VA[:], 0.0), writes=r_VA)
            em.op("dve", lambda e: e.memset(kTp[:], 0.0), writes=r_kT)
            em.op("act", lambda e: e.activation(out=esk[:], in_=snk[:], func=AF.Exp), reads=[r_snk], writes=[r_esk])
            es_hb = xn[0][0][:, 0:16]
            es_tmp, r_es_tmp = xt[2][0], xt[2][1]
            em.op("dve", lambda e: e.tensor_copy(out=es_hb, in_=esk[:]), reads=[r_esk], writes=[xn[0][1]])
            em.op("dve", lambda e: e.tensor_copy(out=es_tmp[:, 0:16], in_=es_hb), reads=[xn[0][1]], writes=[r_es_tmp])
            em.op("dve", lambda e: e.tensor_tensor(out=es_tmp[:, 16:32], in0=esk[:], in1=es_tmp[:, 0:16], op=ALU.subtract),
                  reads=[r_esk, r_es_tmp], writes=[r_es_tmp])
            em.op("dve", lambda e: e.memset(esrhs[:], 0.0), writes=[r_esrhs])
            for g in range(2):
                p_hi, p_lo = 64 * g, 64 * g + 32
                em.op("dve", lambda e: e.tensor_copy(out=esrhs[p_hi:p_hi + 1, :].rearrange("p (h q) -> p h q", q=64),
                                                     in_=es_tmp[p_hi:p_hi + 1, 8 * g:8 * g + 8].unsqueeze(2).to_broadcast([1, 8, 64])),
                      reads=[r_es_tmp], writes=[r_esrhs])
                em.op("dve", lambda e: e.tensor_copy(out=esrhs[p_lo:p_lo + 1, :].rearrange("p (h q) -> p h q", q=64),
                                                     in_=es_tmp[p_lo:p_lo + 1, 16 + 8 * g:16 + 8 * g + 8].unsqueeze(2).to_broadcast([1, 8, 64])),
                      reads=[r_es_tmp], writes=[r_esrhs])
            em.op("dve", lambda e: e.memset(sel2[:], 0.0), writes=[r_sel2])
            em.op("dve", lambda e: e.memset(sel2[0:64, 0:64], 1.0), writes=[r_sel2])
            em.op("dve", lambda e: e.memset(sel2[64:128, 64:128], 1.0), writes=[r_sel2])
            em.op("dve", lambda e: e.tensor_copy(out=wsTb[:], in_=wsTf), reads=[r_wsTf], writes=[r_wsTb])
            em.op("dve", lambda e: e.memset(wsTb[64:128, :, 0:64], 0.0), writes=[r_wsTb])
            for g in range(8):
                b = fb()
                em.op("pe", lambda e, b=b, g=g: e.matmul(psA[:, b, 0:128], lhsT=onesb[:], rhs=wsTb[:, g, :], start=True, stop=True),
                      reads=[r_onesb, r_wsTb], writes=[r_bank[b]])
                em.op("dve", lambda e, b=b, g=g: e.scalar_tensor_tensor(out=Cc[:, g, :], in0=psA[:, b, 0:128], scalar=lnb[:, g:g + 1],
                                                                        in1=bsbc[:, g * 128:(g + 1) * 128], op0=ALU.mult, op1=ALU.add),
                      reads=[r_bank[b], r_lnb, xt[3][1]], writes=[r_Cc])


        winv = win_d.rearrange("(kc p) n -> p kc n", p=128)
        wsov = wso_d.rearrange("(kc p) n -> p kc n", p=128)
        wov = wo_d.rearrange("(kc p) n -> p kc n", p=128)
        slot_ctr = [0]

        def load_unit(kind, arg, dst=None):
            if dst is None:
                i = slot_ctr[0] % NSLOT
                slot_ctr[0] += 1
                t, r = wslot[i]
            else:
                t, r = dst
            if kind == "plain":
                src, ncol = arg
                em.dma("pool", t[:, :, 0:ncol], src, writes=[r])
            elif kind == "perm":
                cb, h0 = arg
                dstv = t[:].rearrange("p kc (hg kv d) -> p kc hg kv d", kv=2, d=64)
                for kv in range(2):
                    c0 = cb + kv * 512 + h0 * 64
                    for hgi in range(4):
                        em.dma("pool", dstv[:, :, hgi, kv, :], winv[:, :, c0 + hgi * 64:c0 + (hgi + 1) * 64],
                               writes=[r], part=(kv + hgi > 0))
            elif kind == "rowperm":
                c0 = arg
                for kv in range(2):
                    src = wao_d[kv * 512:(kv + 1) * 512, c0:c0 + 512].rearrange("(hg d) n -> d hg n", d=64)
                    em.dma("pool", t[kv * 64:(kv + 1) * 64, :, :], src, writes=[r], part=(kv > 0))
            return t, r

        def rstd_chain(s_t, s_r, in_ap, tmp_ap, out_ap, scale, after_accum=False):
            if after_accum:
                em.op("act", lambda e: e.copy(out=s_t[:, 7:8], in_=epsc[:, 0:1]), reads=[r_epsc], writes=[s_r])
            em.op("act", lambda e: e.activation(out=tmp_ap, in_=in_ap, func=AF.Ln, bias=epsc[:, 0:1], scale=scale),
                  reads=[s_r, r_epsc], writes=[s_r])
            em.op("act", lambda e: e.activation(out=out_ap, in_=tmp_ap, func=AF.Exp, scale=-0.5), reads=[s_r], writes=[s_r])

        xload_ctr = [0]
        stat_ctr = [0]

        def stage_A_batch(items):
            n = len(items)
            s_t, s_r = stt[stat_ctr[0] % 4]
            stat_ctr[0] += 1
            bufs = []
            for i, (src_ap, dst_ap, dst_r) in enumerate(items):
                k = xload_ctr[0] % 4
                xload_ctr[0] += 1
                x_t, x_r = xt[k]
                n_t, n_r = xn[i % 2]
                bufs.append((x_t, x_r, n_t, n_r))
                em.dma("sp", x_t[:], src_ap, writes=[x_r])
                em.op("act", lambda e: e.activation(out=sqj[:], in_=x_t[:], func=AF.Square, accum_out=s_t[:, i:i + 1]),
                      reads=[x_r], writes=[r_sqj, s_r])
            rstd_chain(s_t, s_r, s_t[:, 0:n], s_t[:, 2:2 + n], s_t[:, 4:4 + n], 1.0 / D, after_accum=True)
            for i, (x_t, x_r, n_t, n_r) in enumerate(bufs):
                em.op("dve", lambda e: e.scalar_tensor_tensor(out=n_t[:], in0=x_t[:], scalar=s_t[:, 4 + i:5 + i], in1=gbc[:],
                                                              op0=ALU.mult, op1=ALU.mult),
                      reads=[x_r, s_r, r_gbc], writes=[n_r])
            return bufs

        def stage_A_batch_p2(items, bufs):
            for i, (x_t, x_r, n_t, n_r) in enumerate(bufs):
                pT = pT7 if i == 0 else pT6
                for kc in range(8):
                    f = em.op if kc == 7 else em.op_noinc
                    f("pe", lambda e, kc=kc: e.transpose(out=pT[:, kc * 128:(kc + 1) * 128], in_=n_t[:, kc * 128:(kc + 1) * 128],
                                                         identity=identb[:]),
                      reads=[n_r, r_identb], writes=[r_bank[7 - i]])
            for i, (src_ap, dst_ap, dst_r) in enumerate(items):
                pT = pT7 if i == 0 else pT6
                em.op("act", lambda e: e.copy(out=dst_ap, in_=pT.rearrange("p (k t) -> p k t", k=8)),
                      reads=[r_bank[7 - i]], writes=[dst_r])

        def B_mm(t, xT_ap_fn, w_q0, w_q1, w_kv, halo=False):
            b0, b1, b2 = 0, 1, 2
            for kc in range(8):
                last = kc == 7
                if not halo:
                    em.op_noinc("pe", lambda e, kc=kc: e.matmul(psA[:, b0, :].rearrange("p (a b c) -> p a b c", a=4, b=2), lhsT=xT_ap_fn(kc, 0, 128), rhs=permview(w_q0[0], kc)[:, 0:4, :, :],
                                                                start=(kc == 0), stop=(kc == 7)),
                                reads=[xT_ap_fn.res, w_q0[1]], writes=[r_bank[b0]])
                    em.op_noinc("pe", lambda e, kc=kc: e.matmul(psA[:, b1, :].rearrange("p (a b c) -> p a b c", a=4, b=2), lhsT=xT_ap_fn(kc, 0, 128), rhs=permview(w_q1[0], kc)[:, 4:8, :, :],
                                                                start=(kc == 0), stop=(kc == 7)),
                                reads=[xT_ap_fn.res, w_q1[1]], writes=[r_bank[b1]])
                f = em.op if last else em.op_noinc
                f("pe", lambda e, kc=kc: e.matmul(psA[:, b2, 0:256], lhsT=xT_ap_fn(kc, 0, 128), rhs=w_kv[0][:, kc, 0:256],
                                                  start=(kc == 0), stop=(kc == 7)),
                  reads=[xT_ap_fn.res, w_kv[1]], writes=[r_bank[0], r_bank[1], r_bank[2]] if last else [r_bank[b2]])

        def B_evac(t, par, qT_dst, halo=False):
            b2 = 2
            q_t, q_r = qb[0]
            r_t, r_r = rt[0]
            rbanks = [r_bank[2]] if halo else [r_bank[0], r_bank[1], r_bank[2]]
            if halo:
                nh = 2
                ps_ap = psA[:, b2, 0:128].rearrange("p (h d) -> p h d", d=64)
                o_ap = q_t[:, 1024:1152].rearrange("p (h d) -> p h d", d=64)
            else:
                nh = 18
                ps_ap = psA[:, 0:3, :].rearrange("p b c -> p (b c)")[:, 0:1152].rearrange("p (h d) -> p h d", d=64)
                o_ap = q_t[:, 0:1152].rearrange("p (h d) -> p h d", d=64)
            r_ph = em.res("phase")
            va_out = VA[:, t % NVR, :, :].rearrange("p g c -> p (g c)")
            em.op("act", lambda e: e.copy(out=o_ap[:, :, 16:64], in_=ps_ap[:, :, 16:64]), reads=rbanks, writes=[q_r, r_ph])
            if not halo:
                for g in range(2):
                    em.op("act", lambda e, g=g: e.copy(out=va_out[:, g * 192:g * 192 + 64], in_=psA[:, b2, 128 + g * 64:192 + g * 64]),
                          reads=[r_bank[b2]], writes=[r_VA[t % NVR], r_ph])
            cosb = cs[:, t, 0:8].unsqueeze(1).to_broadcast([128, nh, 8])
            sinb = cs[:, t, 8:16].unsqueeze(1).to_broadcast([128, nh, 8])
            x1 = ps_ap[:, :, 0:8]
            x2 = ps_ap[:, :, 8:16]
            T = [r_t[:, j, 0:nh * 8].rearrange("p (h d) -> p h d", d=8) for j in range(4)]
            em.op("dve", lambda e: e.tensor_tensor(out=T[0], in0=x1, in1=cosb, op=ALU.mult), reads=rbanks + [r_cs, r_ph], writes=[r_r])
            em.op("dve", lambda e: e.tensor_tensor(out=T[1], in0=x2, in1=sinb, op=ALU.mult), reads=rbanks + [r_cs], writes=[r_r])
            em.op("dve", lambda e: e.tensor_tensor(out=T[2], in0=x2, in1=cosb, op=ALU.mult), reads=rbanks + [r_cs], writes=[r_r])
            em.op("dve", lambda e: e.tensor_tensor(out=T[3], in0=x1, in1=sinb, op=ALU.mult), reads=rbanks + [r_cs], writes=[r_r])
            em.op("dve", lambda e: e.tensor_tensor(out=o_ap[:, :, 0:8], in0=T[0], in1=T[1], op=ALU.subtract), reads=[r_r], writes=[q_r])
            em.op("dve", lambda e: e.tensor_tensor(out=o_ap[:, :, 8:16], in0=T[2], in1=T[3], op=ALU.add), reads=[r_r], writes=[q_r])
            if halo:
                for g in range(2):
                    em.op("dve", lambda e, g=g: e.tensor_scalar(out=va_out[:, g * 192:g * 192 + 64], in0=psA[:, b2, 128 + g * 64:192 + g * 64],
                                                                scalar1=validt[:, 0:1], scalar2=None, op0=ALU.mult),
                          reads=[r_bank[b2], r_valid], writes=[r_VA[t % NVR]])
        def B_evac_b(t, par, qT_dst, halo=False):
            q_t, q_r = qb[0]
            if not halo:
                for hg in range(8):
                    f = em.op if hg == 7 else em.op_noinc
                    f("pe", lambda e, hg=hg: e.transpose(out=pT6[:, hg * 128:(hg + 1) * 128], in_=q_t[:, hg * 128:(hg + 1) * 128],
                                                         identity=identb[:]),
                      reads=[q_r, r_identb], writes=[r_bank[6]])
                em.op("dve", lambda e: e.tensor_copy(out=qT_dst[0], in_=pT6.rearrange("p (k t) -> p k t", k=8)),
                      reads=[r_bank[6]], writes=[qT_dst[1]])
            em.op("pe", lambda e: e.transpose(out=pT7[:, 0:128], in_=q_t[:, 1024:1152], identity=identb[:]),
                  reads=[q_r, r_identb], writes=[r_bank[7]])
            for g in range(2):
                em.op("dve", lambda e: e.tensor_copy(out=kTp[g * 64:(g + 1) * 64, t % NVR, g, :], in_=pT7[g * 64:(g + 1) * 64, 0:128]),
                      reads=[r_bank[7]], writes=[r_kT[t % NVR]])

        def feat_tile(w, col, rhs_fn, rhs_res, nk=8, bank=None, lhs_fn=None):
            b = fb() if bank is None else bank
            for k in range(nk):
                f = em.op if k == nk - 1 else em.op_noinc
                f("pe", lambda e, k=k: e.matmul(psA[:, b, :], lhsT=(w[0][:, k, col:col + 128] if lhs_fn is None else lhs_fn(k)), rhs=rhs_fn(k),
                                                start=(k == 0), stop=(k == nk - 1)),
                  reads=[w[1]] + rhs_res, writes=[r_bank[b]])
            return b

        def do_stage_A(j):
            prev_stage = em.stage
            em.stage = f"{j}A"
            xT_t, xT_r = xnT[j % 2]
            if j == 0:
                it = [(xh_d, xnTh[:], r_xnTh)]
                stage_A_batch_p2(it, stage_A_batch(it))
            for s0 in (0, 2):
                it = A_items(j, s0)
                stage_A_batch_p2(it, stage_A_batch(it))
            em.stage = prev_stage

        def A_items(j, s0):
            xT_t, xT_r = xnT[j % 2]
            return [(x_d[(4 * j + s) * 128:(4 * j + s + 1) * 128, :], xT_t[:, :, s * 128:(s + 1) * 128], xT_r) for s in (s0, s0 + 1)]

        def A_part(j, s0, part, state):
            prev_stage = em.stage
            em.stage = f"{j}A"
            if part == 1:
                state[s0] = stage_A_batch(A_items(j, s0))
            else:
                stage_A_batch_p2(A_items(j, s0), state[s0])
            em.stage = prev_stage

        w_ga_cur = [None]

        def stage_B_sub(j, s):
            xT_t, xT_r = xnT[j % 2]
            prev = em.stage
            em.stage = f"{j}B"
            t = 4 * j + s + 1

            def xts(kc, lo, hi):
                return xT_t[:, kc, s * 128 + lo:s * 128 + hi]
            xts.res = xT_r
            B_mm(t, xts, wres["q"], wres["q"], wres["kv"])
            B_evac(t, s % 2, None)
            em.stage = f"{j}D"
            for hg in (2 * s, 2 * s + 1):
                bk = 3 + hg % 3
                feat_tile(wres["za0" if hg < 4 else "za1"], (hg % 4) * 128, lambda k: xT_t[:, k, :], [xT_r], bank=bk)
                em.op("act", lambda e: e.activation(out=A1[:, hg, :], in_=psA[:, bk, :], func=AF.Silu),
                      reads=[r_bank[bk]], writes=[r_A1])
            for dt in (2 * s, 2 * s + 1):
                if dt % 4 == 0:
                    w_ga_cur[0] = load_unit("plain", (winv[:, :, 5376 + (dt // 4) * 512:5376 + (dt // 4 + 1) * 512], 512))
                bk = 3 + (dt + 2) % 3
                feat_tile(w_ga_cur[0], (dt % 4) * 128, lambda k: xT_t[:, k, :], [xT_r], bank=bk)
                em.op("act", lambda e: e.activation(out=gaT[:, dt, :], in_=psA[:, bk, :], func=AF.Tanh, bias=bmh[:, dt:dt + 1], scale=0.5),
                      reads=[r_bank[bk], r_bmh], writes=[r_gaT])
            em.stage = f"{j}B"
            B_evac_b(t, s % 2, (A2[:, :, s * 128:(s + 1) * 128], r_A2))
            em.stage = prev

        class _Stop(Exception):
            pass

        _cnt = {}

        def chk(name):
            _cnt[name] = _cnt.get(name, 0) + 1
            if stop == name or stop == f"{name}#{_cnt[name]}":
                raise _Stop()

        try:
          chk("setup")
          do_stage_A(0)
          late_setup()
          chk("A")
          sctr = 0
          for j in range(NBLK):
              xT_t, xT_r = xnT[j % 2]
              sza_t = yaT_t = A1
              qT_t = sT_t = A2
              aT_t = vh_t = yT_t = A3
              if j == 0:
                  load_unit("plain", (winv[:, :, 1024:1280], 256), dst=wres["kv"])
                  load_unit("plain", (winv[:, :, 0:1024], 1024), dst=wres["q"])
                  load_unit("perm", (1280, 0), dst=wres["za0"])
                  load_unit("perm", (1280, 4), dst=wres["za1"])
              w_kv = wres["kv"]
              chk('Bw')
              em.stage = f"{j}B"
              if j == 0:
                  def xth(kc, lo, hi):
                      return xnTh[:, kc, lo:hi]
                  xth.res = r_xnTh
                  B_mm(0, xth, None, None, w_kv, halo=True)
                  B_evac(0, 1, None, halo=True)
                  B_evac_b(0, 1, None, halo=True)

              for s in range(4):
                  if not (j > 0 and s == 0):
                      stage_B_sub(j, s)
              if j == 0:
                  dump("xnT", xT_t, xT_r)
                  dump("qT", A2, r_A2)
                  dump("sza", A1, r_A1)
              chk("B")
              chk("D")
              em.stage = f"{j}C"
              def cpar(c):
                  t = 4 * j + c // 2 + 1
                  if c % 2 == 0:
                      d = dict(s1=(t - 1) % NVR, s2=t % NVR, half=0, h1=(t - 1 == 0), h2=False)
                  else:
                      d = dict(s1=t % NVR, s2=(t - 1) % NVR, half=1, h1=False, h2=(t - 1 == 0))
                  q0 = (c // 2) * 128 + (c % 2) * 64
                  d["qsl"] = slice(q0, q0 + 64)
                  d["p1"] = PT1[c % 2]
                  d["p2"] = PT2[c % 2]
                  d["bPV"] = 4 + 2 * (c % 2)
                  d["bDN"] = 5 + 2 * (c % 2)
                  return d

              def emit_S(c, g):
                  d = cpar(c)
                  ps1, ps2 = 2 * g, 2 * g + 1
                  qsl = d["qsl"]
                  for ps, sl in ((ps1, d["s1"]), (ps2, d["s2"])):
                      em.op("pe", lambda e: e.matmul(psA[:, ps, :].rearrange("p (h q) -> p h q", q=64), lhsT=kTp[:, sl, g, :],
                                                     rhs=qT_t[:, :, qsl], start=True, stop=True),
                            reads=[r_kT[sl], r_A2], writes=[r_bank[ps]])

              def emit_exp(c, g):
                  d = cpar(c)
                  ps1, ps2 = 2 * g, 2 * g + 1
                  p1_t, p1_r = d["p1"]
                  p2_t, p2_r = d["p2"]
                  hs = slice(d["half"] * 64, d["half"] * 64 + 64)
                  em.op("act", lambda e: e.activation(out=p1_t[:, g, :], in_=psA[:, ps1, :], func=AF.Exp, scale=0.125),
                        reads=[r_bank[ps1]], writes=[p1_r])
                  em.op("act", lambda e: e.activation(out=p2_t[hs, g, :], in_=psA[hs, ps2, :], func=AF.Exp, scale=0.125),
                        reads=[r_bank[ps2]], writes=[p2_r])

              def emit_PV(c, g):
                  d = cpar(c)
                  p1_t, p1_r = d["p1"]
                  p2_t, p2_r = d["p2"]
                  bPV, bDN = d["bPV"], d["bDN"]
                  V1, V2, rV1, rV2 = VA[:, d["s1"]], VA[:, d["s2"]], r_VA[d["s1"]], r_VA[d["s2"]]
                  o1 = onesph if d["h1"] else onesp
                  o2 = onesph if d["h2"] else onesp
                  ro1 = r_onesph if d["h1"] else r_onesp
                  ro2 = r_onesph if d["h2"] else r_onesp
                  em.op("pe", lambda e: e.matmul(psA[:, bPV, :], lhsT=V1[:, g, :], rhs=p1_t[:, g, :], start=(g == 0), stop=False),
                        reads=[rV1, p1_r], writes=[r_bank[bPV]])
                  em.op("pe", lambda e: e.matmul(psA[:, bPV, :], lhsT=V2[:, g, :], rhs=p2_t[:, g, :], start=False, stop=(g == 1)),
                        reads=[rV2, p2_r], writes=[r_bank[bPV]])
                  em.op("pe", lambda e: e.matmul(psA[:, bDN, :], lhsT=o1[:, g, :], rhs=p1_t[:, g, :], start=(g == 0), stop=False),
                        reads=[ro1, p1_r], writes=[r_bank[bDN]])
                  em.op("pe", lambda e: e.matmul(psA[:, bDN, :], lhsT=o2[:, g, :], rhs=p2_t[:, g, :], start=False, stop=False),
                        reads=[ro2, p2_r], writes=[r_bank[bDN]])
                  if g == 1:
                      em.op("pe", lambda e: e.matmul(psA[:, bDN, :], lhsT=sel2[:], rhs=esrhs[:], start=False, stop=True),
                            reads=[r_sel2, r_esrhs], writes=[r_bank[bDN]])

              def emit_norm(c):
                  d = cpar(c)
                  bPV, bDN, qsl = d["bPV"], d["bDN"], d["qsl"]
                  dn_t, dn_r = f5["dn"][c % 2]
                  oo_t, oo_r = f5["oo"][c % 2]
                  HC = 384
                  em.op("act", lambda e: e.activation(out=dn_t[:, 0:HC], in_=psA[:, bDN, 0:HC], func=AF.Ln), reads=[r_bank[bDN]], writes=[dn_r])
                  em.op("act", lambda e: e.activation(out=dn_t[:, 0:HC], in_=dn_t[:, 0:HC], func=AF.Exp, scale=-1.0), reads=[dn_r], writes=[dn_r])
                  em.op("dve", lambda e: e.reciprocal(out=dn_t[:, HC:512], in_=psA[:, bDN, HC:512]), reads=[r_bank[bDN], dn_r], writes=[dn_r])
                  em.op("dve", lambda e: e.tensor_tensor(out=oo_t[:], in0=psA[:, bPV, :], in1=dn_t[:], op=ALU.mult),
                        reads=[r_bank[bPV], dn_r], writes=[oo_r])
                  em.op("dve", lambda e: e.tensor_tensor(out=A3[:, :, qsl], in0=oo_t[:].rearrange("p (h q) -> p h q", q=64),
                                                         in1=A1[:, :, qsl], op=ALU.mult),
                        reads=[oo_r, r_A1], writes=[r_A3])

              em.op("dve", lambda e: e.memset(PT2[0][0][64:128, :, :], 0.0), writes=[PT2[0][1]])
              em.op("dve", lambda e: e.memset(PT2[1][0][0:64, :, :], 0.0), writes=[PT2[1][1]])
              def emit_expm(c):
                  d = cpar(c)
                  p1_t, p1_r = d["p1"]
                  p2_t, p2_r = d["p2"]
                  hs = slice(d["half"] * 64, d["half"] * 64 + 64)
                  em.op("act", lambda e: e.activation(out=p1_t[:, :, :], in_=psA[:, 0:4:2, :], func=AF.Exp, scale=0.125),
                        reads=[r_bank[0], r_bank[2]], writes=[p1_r])
                  em.op("act", lambda e: e.activation(out=p2_t[hs, :, :], in_=psA[hs, 1:4:2, :], func=AF.Exp, scale=0.125),
                        reads=[r_bank[1], r_bank[3]], writes=[p2_r])

              emit_S(0, 0)
              emit_S(0, 1)
              emit_expm(0)
              for c in range(8):
                  if c + 1 < 8:
                      emit_S(c + 1, 0)
                      emit_S(c + 1, 1)
                  emit_PV(c, 0)
                  emit_PV(c, 1)
                  if c + 1 < 8:
                      emit_expm(c + 1)
                  emit_norm(c)
              if j == 0:
                  dump("aT", A3, r_A3)
                  if debug:
                      em.dma("sp", dbg_d["VA"], VA[:], reads=r_VA)
                      dumped.extend(r_VA)
              chk("C")
              w_vs = [load_unit("plain", (winv[:, :, 3328 + u * 512:3328 + (u + 1) * 512], 512)) for u in range(2)]
              f1_state = {}

              def F1_batch(s0, part):
                  prev = em.stage
                  em.stage = f"{j}F"
                  if part == 1:
                      s_t, s_r = stt[stat_ctr[0] % 4]
                      stat_ctr[0] += 1
                      f1_state[s0] = (s_t, s_r)
                      banks = {}
                      for s in (s0, s0 + 1):
                          for u in range(2):
                              b = fb()
                              banks[(s, u)] = b
                              for kc in range(8):
                                  f = em.op if kc == 7 else em.op_noinc
                                  f("pe", lambda e, kc=kc: e.matmul(psA[:, b, :], lhsT=xT_t[:, kc, s * 128:(s + 1) * 128], rhs=w_vs[u][0][:, kc, :],
                                                                    start=(kc == 0), stop=(kc == 7)),
                                    reads=[xT_r, w_vs[u][1]], writes=[r_bank[b]])
                      for s in (s0, s0 + 1):
                          gv_t, gv_r = gv[s % 2]
                          for u in range(2):
                              b = banks[(s, u)]
                              em.op("act", lambda e: e.activation(out=gv_t[:, u * 512:(u + 1) * 512], in_=psA[:, b, :], func=AF.Gelu),
                                    reads=[r_bank[b]], writes=[gv_r])
                      for s in (s0, s0 + 1):
                          gv_t, gv_r = gv[s % 2]
                          bs_t, bs_r = bst[s % 2]
                          for u in range(2):
                              em.op("dve", lambda e: e.bn_stats(out=bs_t[:, u, :], in_=gv_t[:, u * 512:(u + 1) * 512]),
                                    reads=[gv_r], writes=[bs_r])
                          em.op("dve", lambda e: e.bn_aggr(out=s_t[:, 2 * (s % 2):2 * (s % 2) + 2], in_=bs_t[:].rearrange("p a b -> p (a b)")),
                                reads=[bs_r], writes=[s_r])
                      var_ap = s_t[:, 0:4].rearrange("p (a b) -> p a b", b=2)[:, :, 1]
                      rstd_chain(s_t, s_r, var_ap, s_t[:, 4:6], s_t[:, 6:8], 1.0)
                  else:
                      s_t, s_r = f1_state[s0]
                      for s in (s0, s0 + 1):
                          gv_t, gv_r = gv[s % 2]
                          vh_dst = vh0[:, s, :] if s < 2 else A3[:, (2 * s):(2 * s + 2), :].rearrange("p a b -> p (a b)")
                          em.op("dve", lambda e: e.tensor_scalar(out=vh_dst, in0=gv_t[:], scalar1=s_t[:, 2 * (s % 2):2 * (s % 2) + 1],
                                                                 scalar2=s_t[:, 6 + s % 2:7 + s % 2], op0=ALU.subtract, op1=ALU.mult),
                                reads=[gv_r, s_r], writes=[r_vh0 if s < 2 else r_A3])
                  em.stage = prev

              F1_batch(0, 1)
              F1_batch(0, 2)
              F1_batch(2, 1)
              em.stage = f"{j}E"
              for u in range(2):
                  w_ao = load_unit("rowperm", u * 512)
                  for dd in range(4):
                      dt = u * 4 + dd
                      bx = feat_tile(w_ao, dd * 128, lambda k: A3[:, k, :], [r_A3])
                      em.op("dve", lambda e: e.scalar_tensor_tensor(out=A1[:, dt, :], in0=gaT[:, dt, :], scalar=1.0, in1=psA[:, bx, :],
                                                                    op0=ALU.add, op1=ALU.mult),
                            reads=[r_bank[bx], r_gaT], writes=[r_A1])
              if j == 0:
                  dump("yaT", A1, r_A1)
              chk("E")
              em.stage = f"{j}F"
              a_state = {}
              F1_batch(2, 2)
              w_u = [None, None]
              w_zs = [None, None]
              for g in range(8):
                  if j + 1 < NBLK:
                      if g == 0:
                          A_part(j + 1, 0, 1, a_state)
                      elif g == 2:
                          A_part(j + 1, 0, 2, a_state)
                      elif g == 3:
                          A_part(j + 1, 2, 1, a_state)
                      elif g == 5:
                          A_part(j + 1, 2, 2, a_state)
                  if g % 4 == 0:
                      w_u[g // 4] = load_unit("plain", (winv[:, :, 2304 + (g // 4) * 512:2304 + (g // 4 + 1) * 512], 512))
                      w_zs[g // 4] = load_unit("plain", (winv[:, :, 4352 + (g // 4) * 512:4352 + (g // 4 + 1) * 512], 512))
                  bu = feat_tile(w_u[g // 4], (g % 4) * 128, lambda k: xT_t[:, k, :], [xT_r])
                  gu_t, gu_r = f5["gu"][g % 2]
                  em.op("act", lambda e, bu=bu, gu_t=gu_t: e.activation(out=gu_t[:], in_=psA[:, bu, :], func=AF.Gelu), reads=[r_bank[bu]], writes=[gu_r])
                  bz = feat_tile(w_zs[g // 4], (g % 4) * 128, lambda k: xT_t[:, k, :], [xT_r])
                  sz_t, sz_r = f5["szs"][g % 2]
                  em.op("act", lambda e, bz=bz, sz_t=sz_t: e.activation(out=sz_t[:], in_=psA[:, bz, :], func=AF.Tanh, scale=0.5), reads=[r_bank[bz]], writes=[sz_r])
                  em.op("dve", lambda e, bz=bz, sz_t=sz_t: e.scalar_tensor_tensor(out=sz_t[:], in0=sz_t[:], scalar=1.0, in1=psA[:, bz, :],
                                                                              op0=ALU.add, op1=ALU.mult),
                        reads=[r_bank[bz], sz_r], writes=[sz_r])
                  bmx = fb()
                  for s in range(4):
                      em.op("pe", lambda e, s=s, g=g, bmx=bmx: e.matmul(
                          psA[:, bmx, s * 128:(s + 1) * 128],
                          lhsT=(vh0[:, s, g * 128:(g + 1) * 128] if s < 2 else
                                A3[:, (2 * s):(2 * s + 2), :].rearrange("p a b -> p (a b)")[:, g * 128:(g + 1) * 128]),
                          rhs=wsTb[:, g, :], start=True, stop=True),
                          reads=[r_vh0 if s < 2 else r_A3, r_wsTb], writes=[r_bank[bmx]])
                  m_t, m_r = f5["msb"][g % 2]
                  em.op("dve", lambda e, g=g, bmx=bmx, m_t=m_t: e.scalar_tensor_tensor(
                      out=m_t[:].rearrange("p (s i) -> p s i", i=128), in0=psA[:, bmx, :].rearrange("p (s i) -> p s i", i=128), scalar=lng[:, g:g + 1],
                      in1=Cc[:, g, :].unsqueeze(1).to_broadcast([128, 4, 128]), op0=ALU.mult, op1=ALU.add),
                      reads=[r_bank[bmx], r_lng, r_Cc], writes=[m_r])
                  em.op("dve", lambda e, m_t=m_t, gu_t=gu_t: e.scalar_tensor_tensor(out=m_t[:], in0=m_t[:], scalar=0.5, in1=gu_t[:], op0=ALU.mult, op1=ALU.mult),
                        reads=[m_r, gu_r], writes=[m_r])
                  em.op("dve", lambda e, g=g, m_t=m_t, sz_t=sz_t: e.tensor_tensor(out=A2[:, g, :], in0=m_t[:], in1=sz_t[:], op=ALU.mult),
                        reads=[m_r, sz_r], writes=[r_A2])
              if j == 0:
                  dump("sT", A2, r_A2)
              chk("F")
              em.stage = f"{j}G"
              h_x = []
              for s in range(4):
                  k = xload_ctr[0] % 4
                  xload_ctr[0] += 1
                  x_t, x_r = xt[k]
                  em.dma("sp", x_t[:], x_d[(4 * j + s) * 128:(4 * j + s + 1) * 128, :], writes=[x_r])
                  h_x.append((x_t, x_r))
              for u in range(2):
                  w_so = load_unit("plain", (wsov[:, :, u * 512:(u + 1) * 512], 512))
                  w_gs = load_unit("plain", (winv[:, :, 6400 + u * 512:6400 + (u + 1) * 512], 512))
                  for dd in range(4):
                      dt = u * 4 + dd
                      by = feat_tile(w_gs, dd * 128, lambda k: xT_t[:, k, :], [xT_r])
                      bx = feat_tile(w_so, dd * 128, lambda k: A2[:, k, :], [r_A2])
                      g_t, g_r = f5["gg"][dt % 2]
                      em.op("act", lambda e, by=by, dt=dt, g_t=g_t: e.activation(out=g_t[:], in_=psA[:, by, :], func=AF.Tanh, bias=bmh[:, 8 + dt:9 + dt], scale=0.5),
                            reads=[r_bank[by], r_bmh], writes=[g_r])
                      em.op("dve", lambda e, bx=bx, g_t=g_t: e.scalar_tensor_tensor(out=g_t[:], in0=g_t[:], scalar=1.0, in1=psA[:, bx, :],
                                                                                 op0=ALU.add, op1=ALU.mult),
                            reads=[r_bank[bx], g_r], writes=[g_r])
                      em.op("dve", lambda e, dt=dt, g_t=g_t: e.tensor_tensor(out=A3[:, dt, :], in0=g_t[:], in1=A1[:, dt, :], op=ALU.add),
                            reads=[g_r, r_A1], writes=[r_A3])
              if j == 0:
                  dump("yT", A3, r_A3)
              chk("G")
              em.stage = f"{j}H"
              w_oo = [load_unit("plain", (wov[:, :, u * 512:(u + 1) * 512], 512)) for u in range(2)]
              for s0 in (0, 2):
                  s_t, s_r = stt[stat_ctr[0] % 4]
                  stat_ctr[0] += 1
                  tiles = []
                  for s in (s0, s0 + 1):
                      t = 4 * j + s
                      x_t, x_r = h_x[s]
                      bb = []
                      for u in range(2):
                          b = fb()
                          bb.append(b)
                          for dt in range(8):
                              f = em.op if dt == 7 else em.op_noinc
                              f("pe", lambda e, dt=dt: e.matmul(psA[:, b, :], lhsT=A3[:, dt, s * 128:(s + 1) * 128], rhs=w_oo[u][0][:, dt, :],
                                                                start=(dt == 0), stop=(dt == 7)),
                                reads=[r_A3, w_oo[u][1]], writes=[r_bank[b]])
                      tiles.append((t, x_t, x_r, bb))
                  for (t, x_t, x_r, bb) in tiles:
                      for u in range(2):
                          em.op("dve", lambda e: e.scalar_tensor_tensor(out=x_t[:, u * 512:(u + 1) * 512], in0=psA[:, bb[u], :], scalar=0.5,
                                                                        in1=x_t[:, u * 512:(u + 1) * 512], op0=ALU.mult, op1=ALU.add),
                                reads=[r_bank[bb[u]]], writes=[x_r])
                  for i, (t, x_t, x_r, bb) in enumerate(tiles):
                      em.op("act", lambda e: e.activation(out=sqj[:], in_=x_t[:], func=AF.Square, accum_out=s_t[:, i:i + 1]),
                            reads=[x_r], writes=[r_sqj, s_r])
                  rstd_chain(s_t, s_r, s_t[:, 0:2], s_t[:, 2:4], s_t[:, 4:6], 1.0 / D, after_accum=True)
                  for i, (t, x_t, x_r, bb) in enumerate(tiles):
                      em.op("dve", lambda e: e.scalar_tensor_tensor(out=x_t[:], in0=x_t[:], scalar=s_t[:, 4 + i:5 + i], in1=fgbc[:],
                                                                    op0=ALU.mult, op1=ALU.mult),
                            reads=[x_r, s_r, r_fgbc], writes=[x_r])
                      em.dma("sp", out_d[t * 128:(t + 1) * 128, :], x_t[:], reads=[x_r])
                  if s0 == 0 and j + 1 < NBLK:
                      stage_B_sub(j + 1, 0)
        except _Stop:
            pass
        em.wait_all("sp", dumped + [r for (_, r) in wslot] + [r for (_, r) in xt])
        em.finish()
    return nc


def rope_table(pos):
    half = 8
    inv_freq = (np.float32(500000.0) ** (-(np.arange(half, dtype=np.float32) * np.float32(2.0)) / np.float32(16))).astype(np.float32)
    ang = (pos.astype(np.float32)[:, None] * inv_freq[None, :]).astype(np.float32)
    return np.concatenate([np.cos(ang), np.sin(ang)], axis=1).astype(np.float32)


def core_inputs(x_core, x_halo, valid, pos0, params):
    ntok = x_core.shape[0]
    nt = ntok // 128
    cs = rope_table(pos0 + np.arange((nt + 1) * 128)).reshape(nt + 1, 128, 16).transpose(1, 0, 2)
    m = dict(params)
    m["x"] = np.ascontiguousarray(x_core, dtype=np.float32)
    m["xh"] = np.ascontiguousarray(x_halo, dtype=np.float32)
    m["cs"] = np.ascontiguousarray(cs, dtype=np.float32)
    m["valid"] = np.full((128, 1), valid, np.float32)
    return m


def shared_params(norm_g, w_in, b_merge, att_sinks, sg_w, sg_b, sg_ln_g, sg_ln_b, w_att_out, w_sg_out, w_o, final_g):
    f = lambda a: np.ascontiguousarray(np.asarray(a), dtype=np.float32)
    sk = np.asarray(att_sinks)[0]
    p = {
        "ident": np.eye(128, dtype=np.float32),
        "w_in": f(w_in[0]), "w_att_out": f(w_att_out[0]), "w_sg_out": f(w_sg_out[0]), "w_o": f(w_o[0]),
        "norm_g": f(norm_g[0]).reshape(1, D), "final_g": f(final_g).reshape(1, D),
        "bm": f(np.asarray(b_merge)[0].reshape(16, 128).T),
        "lng": f(np.asarray(sg_ln_g)[0].reshape(8, 128).T), "lnb": f(np.asarray(sg_ln_b)[0].reshape(8, 128).T),
        "sinks": f(np.tile(sk[None, :], (128, 1))),
        "wsT": f(np.asarray(sg_w)[0].transpose(2, 0, 1)),
        "sgb": f(np.asarray(sg_b)[0].reshape(1, 1024)),
    }
    return p


_NC_CACHE = {}


def kernel(x, norm_g, w_in, b_merge, att_sinks, sg_w, sg_b, sg_ln_g, sg_ln_b, w_att_out, w_sg_out, w_o, final_g):
    x = np.asarray(x)
    B, S, _ = x.shape
    params = shared_params(norm_g, w_in, b_merge, att_sinks, sg_w, sg_b, sg_ln_g, sg_ln_b, w_att_out, w_sg_out, w_o, final_g)
    per_b = NCORES // B
    ntok = S // per_b
    in_maps = []
    for c in range(NCORES):
        b, q = c // per_b, c % per_b
        s0 = q * ntok
        halo = x[b, s0 - 128:s0] if q > 0 else np.zeros((128, D), np.float32)
        in_maps.append(core_inputs(x[b, s0:s0 + ntok], halo, 1.0 if q > 0 else 0.0, s0 - 128, params))
    if "nc" not in _NC_CACHE:
        _NC_CACHE["nc"] = build(NBLK=ntok // 512)
    res = run_bass_kernel_spmd(_NC_CACHE["nc"], in_maps, core_ids=list(range(NCORES)))
    out = np.empty((B, S, D), np.float32)
    for c in range(NCORES):
        b, q = c // per_b, c % per_b
        out[b, q * ntok:(q + 1) * ntok] = res.results[c]["out"]
    return out
```
